# Optimizing a Trainium2 kernel written in Bass

```python
import math
import jax, jax.numpy as jnp
from jax import lax
import numpy as np

D_MODEL = 2048
BATCH = 8
SEQ = 4096
DEPTH = 4

ATTN_HEADS = 8
QK_DIM = 64
V_DIM = 2 * QK_DIM
QK_WIDTH = ATTN_HEADS * 2 * QK_DIM
ATTN_WIDTH = ATTN_HEADS * V_DIM
Q_BLOCK = 128
ROPE_THETA = 10000.0
LRU_WIDTH = D_MODEL - ATTN_WIDTH
LRU_BLOCKS = 8
LRU_BLOCK = LRU_WIDTH // LRU_BLOCKS
LRU_CONV = 4
LRU_C = 8.0
D_FF = 3 * D_MODEL
FFN_CONV = 3
IN_WIDTH = 2 * QK_WIDTH + ATTN_WIDTH + 2 * LRU_WIDTH
EPS = 1e-6

kernel_name = "hymba_diffattn_rglru_convffn_trunk"


def lambda_init_fn(layer_idx):
    return 0.8 - 0.6 * math.exp(-0.3 * layer_idx)


def rmsnorm(x, gain):
    x32 = x.astype(jnp.float32)
    y = x32 * lax.rsqrt(jnp.mean(x32 * x32, axis=-1, keepdims=True) + EPS)
    return (y * gain.astype(jnp.float32)).astype(x.dtype)


def causal_dwconv(x, w, b):
    K = w.shape[0]
    S = x.shape[1]
    xp = jnp.pad(x, ((0, 0), (K - 1, 0), (0, 0)))
    y = b + xp[:, 0:S] * w[0]
    for j in range(1, K):
        y = y + xp[:, j:j + S] * w[j]
    return y


def rope_tables(positions):
    inv_freq = 1.0 / (ROPE_THETA ** (jnp.arange(0, QK_DIM, 2, dtype=jnp.float32) / QK_DIM))
    ang = positions.astype(jnp.float32)[..., None] * inv_freq
    return jnp.cos(ang), jnp.sin(ang)


def apply_rope(t, cos, sin):
    t32 = t.astype(jnp.float32)
    c = cos[:, :, None, None, :]
    s = sin[:, :, None, None, :]
    t1, t2 = jnp.split(t32, 2, axis=-1)
    return jnp.concatenate([t1 * c - t2 * s, t2 * c + t1 * s], axis=-1).astype(t.dtype)


def diff_attention(q, k, v, lam, subln_g, lambda_init):
    B, S = q.shape[0], q.shape[1]
    qh = q.transpose(0, 2, 3, 1, 4)
    kh = k.transpose(0, 2, 3, 1, 4)
    vh = v.transpose(0, 2, 1, 3)
    scale = QK_DIM ** -0.5
    outs = []
    for qb in range(S // Q_BLOCK):
        qs, qe = qb * Q_BLOCK, (qb + 1) * Q_BLOCK
        s = jnp.einsum('bhcqd,bhckd->bhcqk', qh[:, :, :, qs:qe], kh[:, :, :, :qe]).astype(jnp.float32) * scale
        mask = jnp.arange(qs, qe)[:, None] >= jnp.arange(qe)[None, :]
        s = jnp.where(mask, s, -jnp.inf)
        p = jax.nn.softmax(s, axis=-1)
        a = p[:, :, 0] - lam * p[:, :, 1]
        outs.append(jnp.einsum('bhqk,bhkd->bhqd', a, vh[:, :, :qe].astype(jnp.float32)))
    o = jnp.concatenate(outs, axis=2)
    o = rmsnorm(o, subln_g) * (1.0 - lambda_init)
    return o.transpose(0, 2, 1, 3).reshape(B, S, ATTN_WIDTH).astype(v.dtype)


def rglru_branch(xb, gb, conv_w, conv_b, ga_w, ga_b, gx_w, gx_b, lru_lambda, lru_norm):
    B, S = xb.shape[0], xb.shape[1]
    xc = causal_dwconv(xb, conv_w, conv_b)
    xh = xc.reshape(B, S, LRU_BLOCKS, LRU_BLOCK)
    r = jax.nn.sigmoid(jnp.einsum('bshi,hij->bshj', xh, ga_w) + ga_b).reshape(B, S, LRU_WIDTH)
    i = jax.nn.sigmoid(jnp.einsum('bshi,hij->bshj', xh, gx_w) + gx_b).reshape(B, S, LRU_WIDTH)
    r32 = r.astype(jnp.float32)
    log_a = -LRU_C * r32 * jax.nn.softplus(-lru_lambda.astype(jnp.float32))
    a = jnp.exp(log_a)
    mult = jnp.sqrt(-jnp.expm1(2.0 * log_a))
    bterm = mult * (i.astype(jnp.float32) * xc.astype(jnp.float32))

    def combine(left, right):
        a1, b1 = left
        a2, b2 = right
        return a1 * a2, a2 * b1 + b2

    _, h = lax.associative_scan(combine, (a, bterm), axis=1)
    y = h.astype(xb.dtype) * jax.nn.gelu(gb)
    return rmsnorm(y, lru_norm)


def conv_glu_mlp(h, w_up, cw, cb, w_down):
    u = causal_dwconv(h @ w_up, cw, cb)
    g, val = jnp.split(u, 2, axis=-1)
    return (jax.nn.gelu(g) * val) @ w_down


def setup_inputs(seed: int = 0) -> dict:
    key = jax.random.key(seed)
    ks = jax.random.split(key, 24)
    f32 = jnp.float32
    n = lambda k, shp, s: jax.random.normal(k, shp, f32) * s
    x = jax.random.normal(ks[0], (BATCH, SEQ, D_MODEL), f32)
    offs = jax.random.randint(ks[1], (BATCH, 1), 0, 1024, dtype=jnp.int32)
    positions = (offs + jnp.arange(SEQ, dtype=jnp.int32)[None, :]).astype(jnp.int32)
    u = jax.random.uniform(ks[2], (DEPTH, LRU_WIDTH), f32, minval=0.9, maxval=0.999)
    a0 = u ** (1.0 / LRU_C)
    lru_lambda = jnp.log(a0) - jnp.log1p(-a0)
    return {
        "x": x,
        "positions": positions,
        "attn_norm": 1.0 + n(ks[3], (DEPTH, D_MODEL), 0.02),
        "w_in": n(ks[4], (DEPTH, D_MODEL, IN_WIDTH), D_MODEL ** -0.5),
        "lambda_q1": n(ks[5], (DEPTH, QK_DIM), 0.1),
        "lambda_k1": n(ks[6], (DEPTH, QK_DIM), 0.1),
        "lambda_q2": n(ks[7], (DEPTH, QK_DIM), 0.1),
        "lambda_k2": n(ks[8], (DEPTH, QK_DIM), 0.1),
        "subln": 1.0 + n(ks[9], (DEPTH, V_DIM), 0.02),
        "lru_conv_w": n(ks[10], (DEPTH, LRU_CONV, LRU_WIDTH), LRU_CONV ** -0.5),
        "lru_conv_b": n(ks[11], (DEPTH, LRU_WIDTH), 0.01),
        "gate_a_w": n(ks[12], (DEPTH, LRU_BLOCKS, LRU_BLOCK, LRU_BLOCK), LRU_BLOCK ** -0.5),
        "gate_a_b": n(ks[13], (DEPTH, LRU_BLOCKS, LRU_BLOCK), 0.01),
        "gate_x_w": n(ks[14], (DEPTH, LRU_BLOCKS, LRU_BLOCK, LRU_BLOCK), LRU_BLOCK ** -0.5),
        "gate_x_b": n(ks[15], (DEPTH, LRU_BLOCKS, LRU_BLOCK), 0.01),
        "lru_lambda": lru_lambda,
        "lru_norm": 1.0 + n(ks[16], (DEPTH, LRU_WIDTH), 0.02),
        "w_out": n(ks[17], (DEPTH, D_MODEL, D_MODEL), D_MODEL ** -0.5),
        "mlp_norm": 1.0 + n(ks[18], (DEPTH, D_MODEL), 0.02),
        "w_up": n(ks[19], (DEPTH, D_MODEL, 2 * D_FF), D_MODEL ** -0.5),
        "ffn_conv_w": n(ks[20], (DEPTH, FFN_CONV, 2 * D_FF), FFN_CONV ** -0.5),
        "ffn_conv_b": n(ks[21], (DEPTH, 2 * D_FF), 0.01),
        "w_down": n(ks[22], (DEPTH, D_FF, D_MODEL), D_FF ** -0.5),
        "final_norm": 1.0 + n(ks[23], (D_MODEL,), 0.02),
    }


def reference(x, positions, attn_norm, w_in, lambda_q1, lambda_k1, lambda_q2, lambda_k2,
              subln, lru_conv_w, lru_conv_b, gate_a_w, gate_a_b, gate_x_w, gate_x_b,
              lru_lambda, lru_norm, w_out, mlp_norm, w_up, ffn_conv_w, ffn_conv_b,
              w_down, final_norm):
    B, S = x.shape[0], x.shape[1]
    cos, sin = rope_tables(positions)
    splits = [QK_WIDTH, 2 * QK_WIDTH, 2 * QK_WIDTH + ATTN_WIDTH, 2 * QK_WIDTH + ATTN_WIDTH + LRU_WIDTH]
    for l in range(DEPTH):
        lam_init = lambda_init_fn(l)
        h = rmsnorm(x, attn_norm[l])
        proj = h @ w_in[l]
        q, k, v, xb, gb = jnp.split(proj, splits, axis=-1)
        q = apply_rope(q.reshape(B, S, ATTN_HEADS, 2, QK_DIM), cos, sin)
        k = apply_rope(k.reshape(B, S, ATTN_HEADS, 2, QK_DIM), cos, sin)
        v = v.reshape(B, S, ATTN_HEADS, V_DIM)
        lam = (jnp.exp(jnp.sum(lambda_q1[l].astype(jnp.float32) * lambda_k1[l].astype(jnp.float32)))
               - jnp.exp(jnp.sum(lambda_q2[l].astype(jnp.float32) * lambda_k2[l].astype(jnp.float32)))
               + lam_init)
        attn_out = diff_attention(q, k, v, lam, subln[l], lam_init)
        lru_out = rglru_branch(xb, gb, lru_conv_w[l], lru_conv_b[l], gate_a_w[l], gate_a_b[l],
                               gate_x_w[l], gate_x_b[l], lru_lambda[l], lru_norm[l])
        mixed = jnp.concatenate([attn_out.astype(x.dtype), lru_out.astype(x.dtype)], axis=-1)
        x = x + mixed @ w_out[l]
        h = rmsnorm(x, mlp_norm[l])
        x = x + conv_glu_mlp(h, w_up[l], ffn_conv_w[l], ffn_conv_b[l], w_down[l])
    return rmsnorm(x, final_norm)
```

```python
import math
from contextlib import ExitStack

import numpy as np
import concourse.bass as bass
import concourse.mybir as mybir
from concourse.bass_utils import run_bass_kernel_spmd

F32 = mybir.dt.float32
BF16 = mybir.dt.bfloat16
I32 = mybir.dt.int32
AF = mybir.ActivationFunctionType
ALU = mybir.AluOpType
AX = mybir.AxisListType

D = 2048
DC = 16
DEPTH = 4
NH = 8
LW = 1024
DFF = 6144
INW = 5120
EPS = 1e-6
TWO_PI = 2.0 * math.pi

PARAM_SHAPES = {
    "attn_norm": (DEPTH, D), "w_in": (DEPTH, D, INW),
    "lambda_q1": (DEPTH, 64), "lambda_k1": (DEPTH, 64), "lambda_q2": (DEPTH, 64), "lambda_k2": (DEPTH, 64),
    "subln": (DEPTH, 128), "lru_conv_w": (DEPTH, 4, LW), "lru_conv_b": (DEPTH, LW),
    "gate_a_w": (DEPTH, 8, 128, 128), "gate_a_b": (DEPTH, 8, 128),
    "gate_x_w": (DEPTH, 8, 128, 128), "gate_x_b": (DEPTH, 8, 128),
    "lru_lambda": (DEPTH, LW), "lru_norm": (DEPTH, LW), "w_out": (DEPTH, D, D),
    "mlp_norm": (DEPTH, D), "w_up": (DEPTH, D, 2 * DFF), "ffn_conv_w": (DEPTH, 3, 2 * DFF),
    "ffn_conv_b": (DEPTH, 2 * DFF), "w_down": (DEPTH, DFF, D), "final_norm": (D,),
}


def lambda_init_fn(layer_idx):
    return 0.8 - 0.6 * math.exp(-0.3 * layer_idx)


class Sched:
    ENGS = ("pe", "act", "dve", "pool", "sp")
    NCLK = 96

    def __init__(self, nc, es):
        self.nc = nc
        self.es = es
        self.eng = {"pe": nc.tensor, "act": nc.scalar, "dve": nc.vector, "pool": nc.gpsimd, "sp": nc.sync}
        self.sems = []
        self.cnt = []
        self.cur = {}
        self.known = {e: np.zeros(self.NCLK, np.int64) for e in self.ENGS}
        self.snap = {}
        self.buf = {}
        self.nwait = 0
        self.nops = 0
        self.rotate()
        self.rings = {}
        for q, k in (("sp", 12), ("pool", 6), ("act", 4)):
            clks = [self._new_clock(f"d_{q}{i}") for i in range(k)]
            self.rings[q] = {"clks": clks, "i": 0}

    def _new_clock(self, name):
        sem = self.es.enter_context(self.nc.semaphore(name))
        self.sems.append(sem)
        self.cnt.append(0)
        assert len(self.sems) <= self.NCLK
        return len(self.sems) - 1

    def rotate(self):
        n = len(self.sems)
        for e in self.ENGS:
            self.cur[e] = self._new_clock(f"e_{e}{n}")

    def _deps(self, reads, writes, own=None):
        deps = {}

        def add(cv):
            c, v = cv
            if deps.get(c, 0) < v:
                deps[c] = v
        for k in reads:
            b = self.buf.get(k)
            if b is not None and b[0] is not None:
                add(b[0])
            if b is not None and isinstance(k, tuple) and k[0] == "ps":
                for cv in b[1].items():
                    if cv[0] != own:
                        add(cv)
        for k in writes:
            b = self.buf.get(k)
            if b is not None:
                if b[0] is not None:
                    add(b[0])
                for cv in b[1].items():
                    add(cv)
        return deps

    def _wait(self, E, deps):
        kn = self.known[E]
        for c, v in sorted(deps.items(), key=lambda cv: -cv[1]):
            if kn[c] >= v:
                continue
            if E == "pe" and c == self.cur["pe"]:
                continue
            self.eng[E].wait_ge(self.sems[c], int(v))
            self.nwait += 1
            np.maximum(kn, self.snap[(c, v)], out=kn)

    def _commit(self, c, v, E, reads, writes):
        s = self.known[E].copy()
        s[c] = v
        self.snap[(c, v)] = s
        for k in reads:
            b = self.buf.get(k)
            if b is None:
                b = self.buf[k] = [None, {}]
            b[1][c] = v
        for k in writes:
            self.buf[k] = [(c, v), {}]

    def op(self, E, fn, reads=(), writes=()):
        self._wait(E, self._deps(reads, writes, own=self.cur[E]))
        ins = fn(self.eng[E])
        c = self.cur[E]
        self.cnt[c] += 1
        v = self.cnt[c]
        ins.then_inc(self.sems[c], 1)
        self.nops += 1
        self._commit(c, v, E, reads, writes)

    def dma(self, Q, out, in_, reads=(), writes=()):
        ring = self.rings[Q]
        i = ring["i"]
        ring["i"] += 1
        k = len(ring["clks"])
        c = ring["clks"][i % k]
        rnd = i // k
        deps = self._deps(reads, writes)
        if rnd > 0 and deps.get(c, 0) < 16 * rnd:
            deps[c] = 16 * rnd
        self._wait(Q, deps)
        ins = self.eng[Q].dma_start(out=out, in_=in_)
        ins.then_inc(self.sems[c], 16)
        v = 16 * (rnd + 1)
        self.cnt[c] = v
        self._commit(c, v, Q, reads, writes)

    def barrier(self, keep_prefix=("w",)):
        wclks = set(self.rings["pool"]["clks"])
        deps = {c: v for c, v in enumerate(self.cnt) if v > 0 and c not in wclks}
        for E in self.ENGS:
            self._wait(E, dict(deps))
        nb = {}
        for k, b in self.buf.items():
            w = b[0] if (b[0] is not None and b[0][0] in wclks) else None
            r = {c: v for c, v in b[1].items() if c in wclks}
            if w is not None or r:
                nb[k] = [w, r]
        self.buf = nb

    def final_wait(self):
        deps = {c: v for c, v in enumerate(self.cnt) if v > 0}
        for E in self.ENGS:
            self._wait(E, dict(deps))


def build_program(S, layers, first, do_final, dbg=False, stop=None):
    nc = bass.Bass("TRN2", target_bir_lowering=False)
    NB = S // 128
    NT = S // 512
    THM = min(1024, S)
    THF = min(2048, S)

    def din(name, shape, dt=F32):
        return nc.dram_tensor(name, list(shape), dt, kind="ExternalInput").ap()

    x_in = din("x", (S, D))
    pos_in = din("pos", (1, S), I32)
    prm = {k: din(k, shp) for k, shp in PARAM_SHAPES.items()}
    c_ident = din("c_ident", (128, 128))
    c_perm = din("c_perm", (128, 128))
    c_tri = din("c_tri", (128, 128))
    c_vec = din("c_vec", (128, 2))
    out = nc.dram_tensor("out", [S, D], F32, kind="ExternalOutput").ap()

    kind_scr = "ExternalOutput" if dbg else "Internal"

    def dscr(name, shape, dt):
        return nc.dram_tensor(name, list(shape), dt, kind=kind_scr).ap()

    xres = dscr("xres", (S, D), F32)
    qT_d = dscr("qT_d", (NH, 128, S), BF16)
    kT_d = dscr("kT_d", (NH, 128, S), BF16)
    v_d = dscr("v_d", (NH, 128, NB, 128), BF16)
    y_d = dscr("y_d", (8, 128, S), F32)
    at_d = dscr("at_d", (NH, 128, S), BF16)
    o_d = dscr("o_d", (NB, 128, NH, 128), F32)
    act_d = dscr("act_d", (48, 128, S), BF16)
    cos_d = dscr("cos_d", (128, S), F32)
    sin_d = dscr("sin_d", (128, S), F32)

    es = ExitStack()
    with es:
        sc = Sched(nc, es)
        es.enter_context(nc.allow_non_contiguous_dma(reason="small strided parameter loads"))
        es.enter_context(nc.allow_low_precision(reason="bf16 matmul operands, fp32 accumulation"))

        uniq = [0]

        def sb(stack, name, shape, dt=F32):
            uniq[0] += 1
            return stack.enter_context(nc.sbuf_tensor(f"{name}_{uniq[0]}", list(shape), dt))

        PS = [es.enter_context(nc.psum_tensor(f"ps{i}", [128, 1024], F32)) for i in range(4)]

        def bank(i):
            return PS[i // 2][:, (i % 2) * 512:(i % 2) * 512 + 512]

        def bkey(i):
            return ("ps", i)

        ident = sb(es, "ident", (128, 128), BF16)
        perm = sb(es, "perm", (128, 128), BF16)
        tri = sb(es, "tri", (128, 128), BF16)
        ones = sb(es, "ones", (128, 128), BF16)
        identf = sb(es, "identf", (128, 128))
        pstage = [sb(es, f"pstage{i}", (128, 128)) for i in range(2)]
        cvec = sb(es, "cvec", (128, 2))
        RING = 4
        wring = [sb(es, f"wring{i}", (128, 8192), BF16) for i in range(RING)]
        gbc = sb(es, "gbc", (128, D))
        lamt = sb(es, "lamt", (128, 8))
        lq = sb(es, "lq", (128, 4, 64))
        neglam = sb(es, "neglam", (128, 1))
        subG = sb(es, "subG", (128, 128))
        cw = sb(es, "cw", (128, 8, 4))
        cb = sb(es, "cb", (128, 8))
        gab = sb(es, "gab", (128, 8))
        gxb = sb(es, "gxb", (128, 8))
        lru_l = sb(es, "lru_l", (128, 8))
        hc = sb(es, "hc", (128, 8))
        hc2 = sb(es, "hc2", (128, 8))
        lng = sb(es, "lng", (128, 8))
        gaw = sb(es, "gaw", (128, 8, 128), BF16)
        gxw = sb(es, "gxw", (128, 8, 128), BF16)
        fcw = sb(es, "fcw", (128, 96, 3))
        fcb = sb(es, "fcb", (128, 96))
        tmp8 = [sb(es, f"tmp8_{i}", (128, 8)) for i in range(6)]

        dve, act, pe, pool, sp = "dve", "act", "pe", "pool", "sp"

        WL = []
        for l in layers:
            for h in range(S // THM):
                for g in range(10):
                    WL.append(("in", l, g))
            for dq in range(4):
                WL.append(("out", l, dq))
            for h in range(S // THF):
                for g in range(24):
                    WL.append(("up", l, g))
            for tt in range(NT):
                for dh in range(2):
                    for fg in range(6):
                        WL.append(("down", l, dh, fg))
        wstate = {"loaded": 0, "released": 0, "next": 0}

        def w_emit_load(g):
            slot = wring[g % RING]
            key = ("w", g % RING)
            spec = WL[g]
            if spec[0] == "in":
                _, l, gi = spec
                src = prm["w_in"][l].rearrange("(c p) f -> p c f", p=128)
                dst = slot[:].rearrange("p (c f) -> p c f", c=16)
                if gi < 6:
                    sc.dma(pool, dst, src[:, :, gi * 512:(gi + 1) * 512], writes=[key])
                else:
                    j = (gi - 6) * 2
                    sc.dma(pool, dst[:, :, 0:256], src[:, :, 3072 + j * 128:3072 + j * 128 + 256], writes=[key])
                    sc.dma(pool, dst[:, :, 256:512], src[:, :, 4096 + j * 128:4096 + j * 128 + 256], writes=[key])
            elif spec[0] == "out":
                _, l, dq = spec
                src = prm["w_out"][l].rearrange("(c p) f -> p c f", p=128)
                dst = slot[:].rearrange("p (c f) -> p c f", c=16)
                sc.dma(pool, dst, src[:, :, dq * 512:(dq + 1) * 512], writes=[key])
            elif spec[0] == "up":
                _, l, gi = spec
                src = prm["w_up"][l].rearrange("(c p) f -> p c f", p=128)
                dst = slot[:].rearrange("p (c f) -> p c f", c=16)
                j = gi * 2
                sc.dma(pool, dst[:, :, 0:256], src[:, :, j * 128:j * 128 + 256], writes=[key])
                sc.dma(pool, dst[:, :, 256:512], src[:, :, DFF + j * 128:DFF + j * 128 + 256], writes=[key])
            else:
                _, l, dh, fg = spec
                src = prm["w_down"][l].rearrange("(f p) d -> p f d", p=128)
                dst = slot[:].rearrange("p (f d) -> p f d", f=8)
                sc.dma(pool, dst, src[:, fg * 8:(fg + 1) * 8, dh * 1024:(dh + 1) * 1024], writes=[key])

        def w_prefetch():
            lim = min(len(WL), wstate["released"] + RING)
            while wstate["loaded"] < lim:
                w_emit_load(wstate["loaded"])
                wstate["loaded"] += 1

        def w_acquire(n=1):
            g0 = wstate["next"]
            wstate["next"] += n
            assert wstate["released"] == g0
            w_prefetch()
            assert wstate["loaded"] >= g0 + n
            return [(wring[(g0 + i) % RING], ("w", (g0 + i) % RING)) for i in range(n)]

        def w_release(n=1):
            wstate["released"] += n
            w_prefetch()

        sc.dma(pool, ident[:], c_ident, writes=["ident"])
        sc.dma(pool, perm[:], c_perm, writes=["perm"])
        sc.dma(pool, tri[:], c_tri, writes=["tri"])
        sc.dma(sp, cvec[:], c_vec, writes=["cvec"])
        sc.dma(sp, identf[:], c_ident, writes=["identf"])
        sc.op(dve, lambda e: e.memset(ones[:], 1.0), writes=["ones"])
        for i in range(2):
            sc.op(dve, lambda e, i=i: e.memset(pstage[i][:], 0.0), writes=[("pstage", i)])
        w_prefetch()

        with ExitStack() as ph:
            CH = min(S, 2048)
            posi = sb(ph, "posi", (128, CH), I32)
            ang = sb(ph, "ang", (128, CH))
            kf = sb(ph, "kf", (128, CH))
            ki = sb(ph, "ki", (128, CH), I32)
            rr = sb(ph, "rr", (128, CH))
            mm = sb(ph, "mm", (128, CH))
            r2 = sb(ph, "r2", (128, CH))
            C1 = 6.28125
            C2 = TWO_PI - C1
            PI_LO = 3.1415920

            def fix_range(t, tag):
                sc.op(dve, lambda e: e.tensor_single_scalar(out=mm[:], in_=t[:], scalar=math.pi, op=ALU.is_gt),
                      reads=[tag], writes=["mm"])
                sc.op(dve, lambda e: e.scalar_tensor_tensor(out=t[:], in0=mm[:], scalar=-TWO_PI, in1=t[:],
                                                            op0=ALU.mult, op1=ALU.add), reads=["mm", tag], writes=[tag])
                sc.op(dve, lambda e: e.tensor_single_scalar(out=mm[:], in_=t[:], scalar=-math.pi, op=ALU.is_lt),
                      reads=[tag], writes=["mm"])
                sc.op(dve, lambda e: e.scalar_tensor_tensor(out=t[:], in0=mm[:], scalar=TWO_PI, in1=t[:],
                                                            op0=ALU.mult, op1=ALU.add), reads=["mm", tag], writes=[tag])
                sc.op(dve, lambda e: e.tensor_scalar(out=t[:], in0=t[:], scalar1=PI_LO, scalar2=-PI_LO,
                                                     op0=ALU.min, op1=ALU.max), reads=[tag], writes=[tag])

            for ci in range(S // CH):
                cs = slice(ci * CH, (ci + 1) * CH)
                sc.dma(sp, posi[:], pos_in[:, cs].partition_broadcast(128), writes=["posi"])
                sc.op(dve, lambda e: e.tensor_copy(out=ang[:], in_=posi[:]), reads=["posi"], writes=["ang"])
                sc.op(dve, lambda e: e.tensor_scalar(out=ang[:], in0=ang[:], scalar1=cvec[:, 0:1], scalar2=None,
                                                     op0=ALU.mult), reads=["ang", "cvec"], writes=["ang"])
                sc.op(dve, lambda e: e.tensor_scalar(out=kf[:], in0=ang[:], scalar1=1.0 / TWO_PI, scalar2=None,
                                                     op0=ALU.mult), reads=["ang"], writes=["kf"])
                sc.op(dve, lambda e: e.tensor_copy(out=ki[:], in_=kf[:]), reads=["kf"], writes=["ki"])
                sc.op(dve, lambda e: e.tensor_copy(out=kf[:], in_=ki[:]), reads=["ki"], writes=["kf"])
                sc.op(dve, lambda e: e.scalar_tensor_tensor(out=rr[:], in0=kf[:], scalar=-C1, in1=ang[:],
                                                            op0=ALU.mult, op1=ALU.add), reads=["kf", "ang"], writes=["rr"])
                sc.op(dve, lambda e: e.scalar_tensor_tensor(out=rr[:], in0=kf[:], scalar=-C2, in1=rr[:],
                                                            op0=ALU.mult, op1=ALU.add), reads=["kf", "rr"], writes=["rr"])
                fix_range(rr, "rr")
                sc.op(dve, lambda e: e.tensor_scalar(out=r2[:], in0=rr[:], scalar1=math.pi / 2, scalar2=None,
                                                     op0=ALU.add), reads=["rr"], writes=["r2"])
                fix_range(r2, "r2")
                sc.op(act, lambda e: e.activation(out=ang[:], in_=rr[:], func=AF.Sin), reads=["rr"], writes=["ang"])
                sc.op(act, lambda e: e.activation(out=kf[:], in_=r2[:], func=AF.Sin), reads=["r2"], writes=["kf"])
                sc.op(dve, lambda e: e.tensor_scalar(out=ang[:], in0=ang[:], scalar1=cvec[:, 1:2], scalar2=None,
                                                     op0=ALU.mult), reads=["ang", "cvec"], writes=["ang"])
                sc.dma(sp, sin_d[:, cs], ang[:], reads=["ang"], writes=["sin_d"])
                sc.dma(sp, cos_d[:, cs], kf[:], reads=["kf"], writes=["cos_d"])
            sc.barrier()

        def load_layer_params(l):
            lam_init = lambda_init_fn(l)
            for i, nm in enumerate(("lambda_q1", "lambda_k1", "lambda_q2", "lambda_k2")):
                sc.dma(sp, lq[:, i, :], prm[nm][l:l + 1, :].partition_broadcast(128), writes=["lq"])
            sc.dma(sp, subG[:], prm["subln"][l:l + 1, :].partition_broadcast(128), writes=["subG"])
            pcnt = [0]

            def load_T(dst, src, C, key):
                i = pcnt[0] % 2
                pcnt[0] += 1
                sc.dma(sp, pstage[i][0:C, :], src, writes=[("pstage", i)])
                sc.op(pe, lambda e: e.transpose(out=bank(7)[:, 0:128], in_=pstage[i][:, :], identity=identf[:, :]),
                      reads=[("pstage", i), "identf"], writes=[bkey(7)])
                sc.op(dve, lambda e: e.tensor_copy(out=dst, in_=bank(7)[:, 0:C]), reads=[bkey(7)], writes=[key])

            for k in range(4):
                load_T(cw[:, :, k], prm["lru_conv_w"][l, k].rearrange("(c p) -> c p", p=128), 8, "cw")
            for t, nm in ((cb, "lru_conv_b"), (lru_l, "lru_lambda"), (lng, "lru_norm")):
                load_T(t[:], prm[nm][l].rearrange("(c p) -> c p", p=128), 8, nm)
            for t, nm in ((gab, "gate_a_b"), (gxb, "gate_x_b")):
                load_T(t[:], prm[nm][l], 8, nm)
            sc.dma(pool, gaw[:], prm["gate_a_w"][l].rearrange("h i j -> i h j"), writes=["gaw"])
            sc.dma(pool, gxw[:], prm["gate_x_w"][l].rearrange("h i j -> i h j"), writes=["gxw"])
            for k in range(3):
                load_T(fcw[:, :, k], prm["ffn_conv_w"][l, k].rearrange("(c p) -> c p", p=128), 96, "fcw")
            load_T(fcb[:], prm["ffn_conv_b"][l].rearrange("(c p) -> c p", p=128), 96, "fcb")
            sc.op(dve, lambda e: e.tensor_tensor(out=lq[:, 0, :], in0=lq[:, 0, :], in1=lq[:, 1, :], op=ALU.mult),
                  reads=["lq"], writes=["lq"])
            sc.op(dve, lambda e: e.tensor_tensor(out=lq[:, 2, :], in0=lq[:, 2, :], in1=lq[:, 3, :], op=ALU.mult),
                  reads=["lq"], writes=["lq"])
            sc.op(dve, lambda e: e.reduce_sum(out=lamt[:, 0:1], in_=lq[:, 0, :], axis=AX.X), reads=["lq"], writes=["lamt"])
            sc.op(dve, lambda e: e.reduce_sum(out=lamt[:, 1:2], in_=lq[:, 2, :], axis=AX.X), reads=["lq"], writes=["lamt"])
            sc.op(act, lambda e: e.activation(out=lamt[:, 2:4], in_=lamt[:, 0:2], func=AF.Exp), reads=["lamt"], writes=["lamt"])
            sc.op(dve, lambda e: e.scalar_tensor_tensor(out=neglam[:], in0=lamt[:, 3:4], scalar=-lam_init, in1=lamt[:, 2:3],
                                                        op0=ALU.add, op1=ALU.subtract), reads=["lamt"], writes=["neglam"])
            sc.op(dve, lambda e: e.tensor_scalar(out=subG[:], in0=subG[:], scalar1=1.0 - lam_init, scalar2=None, op0=ALU.mult),
                  reads=["subG"], writes=["subG"])
            sc.op(dve, lambda e: e.tensor_scalar(out=gab[:], in0=gab[:], scalar1=0.5, scalar2=None, op0=ALU.mult),
                  reads=["gate_a_b"], writes=["gate_a_b"])
            sc.op(dve, lambda e: e.tensor_scalar(out=gxb[:], in0=gxb[:], scalar1=0.5, scalar2=None, op0=ALU.mult),
                  reads=["gate_x_b"], writes=["gate_x_b"])
            z, w, w2, acc, t5 = tmp8[0], tmp8[1], tmp8[2], tmp8[3], tmp8[4]
            sc.op(act, lambda e: e.activation(out=z[:], in_=lru_l[:], func=AF.Exp, scale=-1.0), reads=["lru_lambda"], writes=["t0"])
            sc.op(dve, lambda e: e.tensor_scalar(out=w[:], in0=z[:], scalar1=2.0, scalar2=None, op0=ALU.add), reads=["t0"], writes=["t1"])
            sc.op(dve, lambda e: e.reciprocal(out=w[:], in_=w[:]), reads=["t1"], writes=["t1"])
            sc.op(dve, lambda e: e.tensor_tensor(out=w[:], in0=w[:], in1=z[:], op=ALU.mult), reads=["t1", "t0"], writes=["t1"])
            sc.op(dve, lambda e: e.tensor_tensor(out=w2[:], in0=w[:], in1=w[:], op=ALU.mult), reads=["t1"], writes=["t2"])
            sc.op(dve, lambda e: e.tensor_scalar(out=acc[:], in0=w2[:], scalar1=1.0 / 13.0, scalar2=1.0 / 11.0,
                                                 op0=ALU.mult, op1=ALU.add), reads=["t2"], writes=["t3"])
            for cf in (1.0 / 9.0, 1.0 / 7.0, 1.0 / 5.0, 1.0 / 3.0, 1.0):
                sc.op(dve, lambda e: e.tensor_tensor(out=acc[:], in0=acc[:], in1=w2[:], op=ALU.mult), reads=["t3", "t2"], writes=["t3"])
                sc.op(dve, lambda e, cf=cf: e.tensor_scalar(out=acc[:], in0=acc[:], scalar1=cf, scalar2=None, op0=ALU.add),
                      reads=["t3"], writes=["t3"])
            sc.op(dve, lambda e: e.tensor_tensor(out=acc[:], in0=acc[:], in1=w[:], op=ALU.mult), reads=["t3", "t1"], writes=["t3"])
            sc.op(dve, lambda e: e.tensor_scalar(out=hc[:], in0=acc[:], scalar1=-8.0, scalar2=None, op0=ALU.mult), reads=["t3"], writes=["hc"])
            sc.op(dve, lambda e: e.tensor_scalar(out=hc2[:], in0=acc[:], scalar1=-16.0, scalar2=None, op0=ALU.mult), reads=["t3"], writes=["hc2"])

        def norm_transpose(x_src, xkey, gname, l, hT, half, tag, TH):
            with ExitStack() as ph:
                _norm_transpose(ph, x_src, xkey, gname, l, hT, half, tag, TH)
                sc.barrier()

        def _norm_transpose(ph, x_src, xkey, gname, l, hT, half, tag, TH):
            BPH = TH // 128
            xt = [sb(ph, f"xt{tag}{i}", (128, D)) for i in range(2)]
            xn = [sb(ph, f"xn{tag}{i}", (128, D), BF16) for i in range(2)]
            st = [sb(ph, f"st{tag}{i}", (128, 4)) for i in range(2)]
            sc.dma(sp, gbc[:], prm[gname][l:l + 1, :].partition_broadcast(128), writes=["gbc"])
            b0 = half * BPH

            def load(bi):
                b = b0 + bi
                sc.dma(sp, xt[bi % 2][:], x_src[b * 128:(b + 1) * 128, :], reads=[(xkey, b)], writes=[("xt", bi % 2)])
            load(0)
            for bi in range(BPH):
                i2 = bi % 2
                if bi + 1 < BPH:
                    load(bi + 1)
                X, XN, ST = xt[i2], xn[i2], st[i2]
                kx, kn, ks = ("xt", i2), ("xn", i2), ("st", i2)
                sc.op(act, lambda e: e.activation(out=XN[:], in_=X[:], func=AF.Square, accum_out=ST[:, 0:1]),
                      reads=[kx], writes=[kn, ks])
                sc.op(dve, lambda e: e.tensor_scalar(out=ST[:, 1:2], in0=ST[:, 0:1], scalar1=1.0 / D, scalar2=EPS,
                                                     op0=ALU.mult, op1=ALU.add), reads=[ks], writes=[ks])
                sc.op(act, lambda e: e.activation(out=ST[:, 2:3], in_=ST[:, 1:2], func=AF.Sqrt), reads=[ks], writes=[ks])
                sc.op(dve, lambda e: e.reciprocal(out=ST[:, 3:4], in_=ST[:, 2:3]), reads=[ks], writes=[ks])
                sc.op(dve, lambda e: e.scalar_tensor_tensor(out=XN[:], in0=X[:], scalar=ST[:, 3:4], in1=gbc[:],
                                                            op0=ALU.mult, op1=ALU.mult), reads=[kx, ks, "gbc"], writes=[kn])
                pb = 2 * i2
                pv = PS[pb // 2][:].bitcast(BF16)

                def tr(e):
                    for c in range(DC):
                        ins = e.transpose(out=pv[:, c * 128:(c + 1) * 128], in_=XN[:, c * 128:(c + 1) * 128], identity=ident[:])
                    return ins
                sc.op(pe, tr, reads=[kn, "ident"], writes=[bkey(pb), bkey(pb + 1)])
                dst = hT[:, :, bi * 128:(bi + 1) * 128]
                srcv = pv.rearrange("p (c t) -> p c t", c=DC)
                ev = act if bi % 2 == 0 else dve
                if ev == act:
                    sc.op(act, lambda e: e.activation(out=dst, in_=srcv, func=AF.Identity),
                          reads=[bkey(pb), bkey(pb + 1)], writes=[("hT", bi // 4)])
                else:
                    sc.op(dve, lambda e: e.tensor_copy(out=dst, in_=srcv),
                          reads=[bkey(pb), bkey(pb + 1)], writes=[("hT", bi // 4)])

        def phase_mixer_proj(l, x_src, xkey):
            TH = THM
            NHALF, TPH, BPH = S // TH, TH // 512, TH // 128
            for half in range(NHALF):
                with ExitStack() as ph:
                    hT = sb(ph, "hT", (128, DC, TH), BF16)
                    norm_transpose(x_src, xkey, "attn_norm", l, hT, half, "a", TH)
                    cosT = sb(ph, "cosT", (128, TH))
                    sinT = sb(ph, "sinT", (128, TH))
                    hs_ = slice(half * TH, (half + 1) * TH)
                    sc.dma(sp, cosT[:], cos_d[:, hs_], reads=["cos_d"], writes=["cosT"])
                    sc.dma(sp, sinT[:], sin_d[:, hs_], reads=["sin_d"], writes=["sinT"])
                    qb = [sb(ph, f"qb{i}", (128, 512), BF16) for i in range(2)]
                    t1 = [sb(ph, f"t1_{i}", (128, 512)) for i in range(2)]
                    t2 = [sb(ph, f"t2_{i}", (128, 512)) for i in range(2)]
                    ob = [sb(ph, f"ob{i}", (128, 512), BF16) for i in range(3)]
                    U = [sb(ph, f"U{i}", (128, 515)) for i in range(2)]
                    if half == 0:
                        pass
                    xc = [sb(ph, f"xc{i}", (128, 512)) for i in range(2)]
                    xc16 = [sb(ph, f"xc16_{i}", (128, 512), BF16) for i in range(2)]
                    ta = [sb(ph, f"ta{i}", (128, 512)) for i in range(2)]
                    ti = [sb(ph, f"ti{i}", (128, 512)) for i in range(2)]
                    av = [sb(ph, f"av{i}", (128, 512)) for i in range(2)]
                    mv = [sb(ph, f"mv{i}", (128, 512)) for i in range(2)]
                    hsb = [sb(ph, f"hsb{i}", (128, 512)) for i in range(2)]
                    gg = [sb(ph, f"gg{i}", (128, 512)) for i in range(4)]
                    lit = [0]
                    ys = [sb(ph, f"ys{i}", (128, 512)) for i in range(3)]
                    cnt = {"ob": 0, "ys": 0, "rope": 0, "bank": 0}
                    import os as _os
                    _ms = int(_os.environ.get("MIXSTOP", "10"))
                    for g in range(10):
                        if g >= _ms:
                            break
                        (slot, wkey), = w_acquire(1)
                        W = slot[:].rearrange("p (c f) -> p c f", c=DC)
                        if g < 4:
                            dstT = qT_d if g < 2 else kT_d
                            for i in range(4):
                                hd = (g % 2) * 4 + i
                                for tl in range(TPH):
                                    tok = slice(tl * 512, (tl + 1) * 512)
                                    gt = slice(half * TH + tl * 512, half * TH + (tl + 1) * 512)
                                    r = cnt["rope"] % 2
                                    cnt["rope"] += 1
                                    pb, pb2 = r, 2 + r
                                    pbk, pb2k = bkey(pb), bkey(pb2)

                                    def mmg(e, i=i, tok=tok, pb=pb):
                                        for c in range(DC):
                                            ins = e.matmul(bank(pb), lhsT=W[:, c, i * 128:(i + 1) * 128], rhs=hT[:, c, tok],
                                                           start=(c == 0), stop=(c == DC - 1))
                                        return ins
                                    sc.op(pe, mmg, reads=[wkey, ("hT", tl)], writes=[pbk])
                                    sc.op(act, lambda e, r=r, pb=pb: e.activation(out=qb[r][:], in_=bank(pb), func=AF.Identity),
                                          reads=[pbk], writes=[("qb", r)])
                                    sc.op(pe, lambda e, r=r, pb2=pb2: e.matmul(bank(pb2), lhsT=perm[:], rhs=qb[r][:], start=True, stop=True),
                                          reads=[("qb", r), "perm"], writes=[pb2k])
                                    sc.op(dve, lambda e, r=r, pb=pb, tok=tok: e.tensor_tensor(out=t1[r][:], in0=bank(pb), in1=cosT[:, tok], op=ALU.mult),
                                          reads=[pbk, "cosT"], writes=[("t1", r)])
                                    sc.op(dve, lambda e, r=r, pb2=pb2, tok=tok: e.tensor_tensor(out=t2[r][:], in0=bank(pb2), in1=sinT[:, tok], op=ALU.mult),
                                          reads=[pb2k, "sinT"], writes=[("t2", r)])
                                    o = cnt["ob"] % 3
                                    cnt["ob"] += 1
                                    sc.op(dve if _os.environ.get("NOPOOL") else pool, lambda e, r=r, o=o: e.tensor_tensor(out=ob[o][:], in0=t1[r][:], in1=t2[r][:], op=ALU.add),
                                          reads=[("t1", r), ("t2", r)], writes=[("ob", o)])
                                    sc.dma(sp, dstT[hd, :, gt], ob[o][:], reads=[("ob", o)], writes=[("qk_d", g < 2, hd)])
                        elif g < 6:
                            for bi in range(BPH):
                                b = half * BPH + bi
                                pb = 4 + (bi % 2)
                                pbk = bkey(pb)

                                def mmv(e, bi=bi, pb=pb):
                                    for c in range(DC):
                                        ins = e.matmul(bank(pb), lhsT=hT[:, c, bi * 128:(bi + 1) * 128], rhs=W[:, c, :],
                                                       start=(c == 0), stop=(c == DC - 1))
                                    return ins
                                sc.op(pe, mmv, reads=[wkey, ("hT", bi // 4)], writes=[pbk])
                                o = cnt["ob"] % 3
                                cnt["ob"] += 1
                                if bi % 2 == 0:
                                    sc.op(act, lambda e, o=o, pb=pb: e.activation(out=ob[o][:], in_=bank(pb), func=AF.Identity),
                                          reads=[pbk], writes=[("ob", o)])
                                else:
                                    sc.op(dve, lambda e, o=o, pb=pb: e.tensor_copy(out=ob[o][:], in_=bank(pb)),
                                          reads=[pbk], writes=[("ob", o)])
                                h0 = (g - 4) * 4
                                sc.dma(sp, v_d[h0:h0 + 4, :, b, :].rearrange("h p d -> p h d"),
                                       ob[o][:].rearrange("p (h d) -> p h d", h=4), reads=[("ob", o)], writes=[("v_d", g)])
                        else:
                            j0 = (g - 6) * 2
                            for tl in range(TPH):
                                tok = slice(tl * 512, (tl + 1) * 512)
                                gt = slice(half * TH + tl * 512, half * TH + (tl + 1) * 512)
                                first_tile = (half == 0 and tl == 0)
                                for q in range(4):
                                    def mml(e, q=q, tok=tok):
                                        for c in range(DC):
                                            ins = e.matmul(bank(q), lhsT=W[:, c, q * 128:(q + 1) * 128], rhs=hT[:, c, tok],
                                                           start=(c == 0), stop=(c == DC - 1))
                                        return ins
                                    sc.op(pe, mml, reads=[wkey, ("hT", tl)], writes=[bkey(q)])
                                par = (lit[0] % 2) * 2
                                lit[0] += 1
                                for jj in range(2):
                                    sc.op(act, lambda e, jj=jj, par=par: e.activation(out=gg[par + jj][:], in_=bank(2 + jj), func=AF.Gelu),
                                          reads=[bkey(2 + jj)], writes=[("gg", par + jj)])
                                for jj in range(2):
                                    j = j0 + jj
                                    Uj = U[jj]
                                    uk = ("U", jj)
                                    if tl == 0:
                                        if first_tile:
                                            sc.op(pool, lambda e, Uj=Uj: e.memset(Uj[:, 0:3], 0.0), writes=[uk])
                                        else:
                                            sc.dma(sp, Uj[:, 0:3], halo_d[j], reads=[("halo_d", j)], writes=[uk])
                                    sc.op(act, lambda e, Uj=Uj, jj=jj: e.activation(out=Uj[:, 3:515], in_=bank(jj), func=AF.Identity),
                                          reads=[bkey(jj)], writes=[uk])
                                    XC = xc[jj]
                                    xk = ("xc", jj)
                                    sc.op(dve, lambda e, Uj=Uj, XC=XC, j=j: e.tensor_scalar(out=XC[:], in0=Uj[:, 3:515], scalar1=cw[:, j, 3:4], scalar2=cb[:, j:j + 1],
                                                                                            op0=ALU.mult, op1=ALU.add), reads=[uk, "cw", "lru_conv_b"], writes=[xk])
                                    for k in range(3):
                                        sc.op(dve, lambda e, Uj=Uj, XC=XC, j=j, k=k: e.scalar_tensor_tensor(out=XC[:], in0=Uj[:, k:k + 512], scalar=cw[:, j, k:k + 1], in1=XC[:],
                                                                                                          op0=ALU.mult, op1=ALU.add), reads=[uk, xk, "cw"], writes=[xk])
                                    sc.op(pool, lambda e, Uj=Uj: e.tensor_copy(out=Uj[:, 0:3], in_=Uj[:, 512:515]), reads=[uk], writes=[uk])
                                    if tl == TPH - 1 and half + 1 < NHALF:
                                        sc.dma(sp, halo_d[j], Uj[:, 0:3], reads=[uk], writes=[("halo_d", j)])
                                    sc.op(pool, lambda e, XC=XC, jj=jj: e.tensor_copy(out=xc16[jj][:], in_=XC[:]), reads=[xk], writes=[("xc16", jj)])
                                    sc.op(pe, lambda e, jj=jj, j=j: e.matmul(bank(4 + jj), lhsT=gaw[:, j, :], rhs=xc16[jj][:], start=True, stop=True),
                                          reads=[("xc16", jj), "gaw"], writes=[bkey(4 + jj)])
                                    sc.op(pe, lambda e, jj=jj, j=j: e.matmul(bank(6 + jj), lhsT=gxw[:, j, :], rhs=xc16[jj][:], start=True, stop=True),
                                          reads=[("xc16", jj), "gxw"], writes=[bkey(6 + jj)])
                                for jj in range(2):
                                    j = j0 + jj
                                    sc.op(act, lambda e, jj=jj, j=j: e.activation(out=ta[jj][:], in_=bank(4 + jj), func=AF.Tanh, bias=gab[:, j:j + 1], scale=0.5),
                                          reads=[bkey(4 + jj), "gate_a_b"], writes=[("ta", jj)])
                                    sc.op(act, lambda e, jj=jj, j=j: e.activation(out=ti[jj][:], in_=bank(6 + jj), func=AF.Tanh, bias=gxb[:, j:j + 1], scale=0.5),
                                          reads=[bkey(6 + jj), "gate_x_b"], writes=[("ti", jj)])
                                    sc.op(act, lambda e, jj=jj, j=j: e.activation(out=av[jj][:], in_=ta[jj][:], func=AF.Exp, bias=hc[:, j:j + 1], scale=hc[:, j:j + 1]),
                                          reads=[("ta", jj), "hc"], writes=[("av", jj)])
                                    sc.op(act, lambda e, jj=jj, j=j: e.activation(out=mv[jj][:], in_=ta[jj][:], func=AF.Exp, bias=hc2[:, j:j + 1], scale=hc2[:, j:j + 1]),
                                          reads=[("ta", jj), "hc2"], writes=[("mv", jj)])
                                for jj in range(2):
                                    sc.op(act, lambda e, jj=jj: e.activation(out=mv[jj][:], in_=mv[jj][:], func=AF.Sqrt, bias=1.0, scale=-1.0),
                                          reads=[("mv", jj)], writes=[("mv", jj)])
                                for jj in range(2):
                                    j = j0 + jj
                                    XC = xc[jj]
                                    xk = ("xc", jj)
                                    sc.op(dve, lambda e, jj=jj, XC=XC: e.scalar_tensor_tensor(out=ti[jj][:], in0=ti[jj][:], scalar=1.0, in1=XC[:], op0=ALU.add, op1=ALU.mult),
                                          reads=[("ti", jj), xk], writes=[("ti", jj)])
                                    sc.op(dve, lambda e, jj=jj: e.scalar_tensor_tensor(out=ti[jj][:], in0=ti[jj][:], scalar=0.5, in1=mv[jj][:], op0=ALU.mult, op1=ALU.mult),
                                          reads=[("ti", jj), ("mv", jj)], writes=[("ti", jj)])
                                    init = 0.0 if first_tile else hcar[:, j:j + 1]
                                    sc.op(dve, lambda e, jj=jj, init=init: e.tensor_tensor_scan(out=hsb[jj][:], data0=av[jj][:], data1=ti[jj][:], initial=init, op0=ALU.mult, op1=ALU.add),
                                          reads=[("av", jj), ("ti", jj), ("hcar", j)], writes=[("hsb", jj)])
                                    sc.op(dve, lambda e, jj=jj, j=j: e.tensor_copy(out=hcar[:, j:j + 1], in_=hsb[jj][:, 511:512]),
                                          reads=[("hsb", jj)], writes=[("hcar", j)])
                                    o = cnt["ys"] % 3
                                    cnt["ys"] += 1
                                    sc.op(pool, lambda e, jj=jj, o=o, par=par: e.tensor_tensor(out=ys[o][:], in0=hsb[jj][:], in1=gg[par + jj][:], op=ALU.mult),
                                          reads=[("hsb", jj), ("gg", par + jj)], writes=[("ys", o)])
                                    sc.dma(sp, y_d[j, :, gt], ys[o][:], reads=[("ys", o)], writes=[("y_d", j)])
                        w_release(1)
                    sc.barrier(keep_prefix=("w", "ps", "hcar", "halo_d"))

        def phase_attention(l):
            with ExitStack() as ph:
                kT = [sb(ph, f"kT{i}", (128, S), BF16) for i in range(2)]
                qz = [[sb(ph, f"qz{i}_{c}", (128, S), BF16) for c in range(2)] for i in range(2)]
                va = [sb(ph, f"va{i}", (128, NB, 129), BF16) for i in range(2)]
                NPB = 4
                pT = [sb(ph, f"pT{i}", (128, 512), BF16) for i in range(NPB)]
                oall = sb(ph, "oall", (128, NB, 128))
                rec = [sb(ph, f"rec{i}", (128, 4)) for i in range(2)]
                o1 = [sb(ph, f"o1_{i}", (128, 128)) for i in range(2)]
                for i in range(2):
                    sc.op(dve, lambda e, i=i: e.memset(va[i][:, :, 128:129], 1.0), writes=[("va", i)])
                    sc.op(dve, lambda e, i=i: e.memset(qz[i][0][64:128, :], 0.0), writes=[("qT", i)])
                    sc.op(dve, lambda e, i=i: e.memset(qz[i][1][0:64, :], 0.0), writes=[("qT", i)])

                def load_head(h):
                    i = h % 2
                    sc.dma(sp, kT[i][:], kT_d[h], reads=[("qk_d", False, h)], writes=[("kT", i)])
                    sc.dma(sp, qz[i][0][0:64, :], qT_d[h, 0:64, :], reads=[("qk_d", True, h)], writes=[("qT", i)])
                    sc.dma(sp, qz[i][1][64:128, :], qT_d[h, 64:128, :], reads=[("qk_d", True, h)], writes=[("qT", i)])
                    sc.dma(sp, va[i][:, :, 0:128], v_d[h], reads=[("v_d", 4 + h // 4)], writes=[("va", i)])

                units = [(h, qt, kb, c) for h in range(NH) for qt in range(NT) for kb in range(4 * qt + 4) for c in range(2)]
                NU = len(units)
                fin = [0]

                def emit_qk(u):
                    h, qt, kb, c = units[u]
                    hi = h % 2
                    j = kb - 4 * qt
                    qlo = j * 128 if j >= 0 else 0
                    bk = u % NPB
                    sc.op(pe, lambda e: e.matmul(bank(bk)[:, qlo:512],
                                                 lhsT=kT[hi][:, kb * 128:(kb + 1) * 128],
                                                 rhs=qz[hi][c][:, qt * 512 + qlo:(qt + 1) * 512],
                                                 start=True, stop=True),
                          reads=[("kT", hi), ("qT", hi)], writes=[bkey(bk)])

                def emit_rest(u):
                    h, qt, kb, c = units[u]
                    if qt == 0 and kb == 0 and c == 0 and h + 1 < NH:
                        load_head(h + 1)
                    hi = h % 2
                    j = kb - 4 * qt
                    diag = j >= 0
                    qlo = j * 128 if diag else 0
                    bk = u % NPB
                    P = pT[bk]
                    pk = ("pT", bk)
                    sc.op(act, lambda e: e.activation(out=P[:, qlo:512], in_=bank(bk)[:, qlo:512], func=AF.Exp, scale=0.125),
                          reads=[bkey(bk)], writes=[pk])
                    if diag:
                        sc.op(pool, lambda e: e.tensor_tensor(out=P[:, qlo:qlo + 128], in0=P[:, qlo:qlo + 128], in1=tri[:], op=ALU.mult),
                              reads=[pk, "tri"], writes=[pk])
                    qb0 = j if diag else 0
                    V = va[hi]

                    def pv(e):
                        for qb in range(qb0, 4):
                            last = (kb == 4 * qt + qb) and c == 1
                            ins = e.matmul(bank(4 + qb)[:, c * 129:(c + 1) * 129],
                                           lhsT=P[:, qb * 128:(qb + 1) * 128], rhs=V[:, kb, :],
                                           start=(kb == 0 and c == 0), stop=last, skip_group_check=True)
                        return ins
                    sc.op(pe, pv, reads=[pk, ("va", hi)], writes=[bkey(4 + qb) for qb in range(qb0, 4)])
                    if diag and c == 1:
                        qb = j
                        gb_ = qt * 4 + qb
                        f2 = fin[0] % 2
                        fin[0] += 1
                        ob_ = bank(4 + qb)
                        ok = bkey(4 + qb)
                        R = rec[f2]
                        rk = ("rec", f2)
                        sums = ob_[:, 0:258].rearrange("p (c d) -> p c d", c=2)[:, :, 128]
                        sc.op(dve, lambda e: e.reciprocal(out=R[:, 0:2], in_=sums), reads=[ok], writes=[rk])
                        sc.op(dve, lambda e: e.tensor_scalar(out=R[:, 2:3], in0=R[:, 1:2], scalar1=neglam[:, 0:1], scalar2=None, op0=ALU.mult),
                              reads=[rk, "neglam"], writes=[rk])
                        sc.op(dve, lambda e: e.tensor_scalar(out=o1[f2][:], in0=ob_[:, 0:128], scalar1=R[:, 0:1], scalar2=None, op0=ALU.mult),
                              reads=[ok, rk], writes=[("o1", f2)])
                        sc.op(dve, lambda e: e.scalar_tensor_tensor(out=oall[:, gb_, :], in0=ob_[:, 129:257], scalar=R[:, 2:3], in1=o1[f2][:],
                                                                    op0=ALU.mult, op1=ALU.add),
                              reads=[ok, rk, ("o1", f2)], writes=[("oall", qt)])
                        if qb == 3:
                            sc.dma(sp, o_d[qt * 4:(qt + 1) * 4, :, h, :].rearrange("b p d -> p b d"), oall[:, qt * 4:(qt + 1) * 4, :],
                                   reads=[("oall", qt)], writes=[("o_d", qt)])

                DEPTH_PIPE = NPB - 1
                load_head(0)
                for u in range(min(DEPTH_PIPE, NU)):
                    emit_qk(u)
                for u in range(NU):
                    if u + DEPTH_PIPE < NU:
                        emit_qk(u + DEPTH_PIPE)
                    emit_rest(u)
                w_prefetch()
                sc.barrier()

        def phase_out_proj(l, x_src, xkey):
            with ExitStack() as ph:
                mixT = [sb(ph, f"mixT{i}", (128, 16, 512), BF16) for i in range(2)]
                yt = sb(ph, "yt", (128, 8, 512))
                ysq = sb(ph, "ysq", (128, 8, 512), BF16)
                sd = sb(ph, "sd", (128, 512))
                xb_ = [sb(ph, f"xb{i}", (128, D)) for i in range(2)]
                obf = [sb(ph, f"obf{i}", (128, 8, 128)) for i in range(2)]
                sq = sb(ph, "sq", (128, 8, 128))
                onb = [sb(ph, f"onb{i}", (128, 8, 128), BF16) for i in range(2)]
                rs8 = [sb(ph, f"rs8_{i}", (128, 8)) for i in range(2)]
                slots = w_acquire(4)
                ncnt = [0]

                def prep_tile(tt):
                    i = tt % 2
                    M = mixT[i]
                    tok = slice(tt * 512, (tt + 1) * 512)
                    sc.dma(sp, yt[:], y_d[:, :, tok].rearrange("j p t -> p j t"),
                           reads=[("y_d", j) for j in range(8)], writes=["yt"])
                    for tb in range(4):
                        b = tt * 4 + tb
                        n2 = ncnt[0] % 2
                        ncnt[0] += 1
                        OB, ON, RS = obf[n2], onb[n2], rs8[n2]
                        kob, kon, krs = ("obf", n2), ("onb", n2), ("rs8", n2)
                        sc.dma(sp, OB[:], o_d[b], reads=[("o_d", tt)], writes=[kob])
                        sc.op(dve, lambda e, OB=OB: e.tensor_tensor(out=sq[:], in0=OB[:], in1=OB[:], op=ALU.mult), reads=[kob], writes=["sq"])
                        sc.op(dve, lambda e, RS=RS: e.reduce_sum(out=RS[:], in_=sq[:], axis=AX.X), reads=["sq"], writes=[krs])
                        sc.op(dve, lambda e, RS=RS: e.tensor_scalar(out=RS[:], in0=RS[:], scalar1=1.0 / 128.0, scalar2=EPS, op0=ALU.mult, op1=ALU.add),
                              reads=[krs], writes=[krs])
                        sc.op(act, lambda e, RS=RS: e.activation(out=RS[:], in_=RS[:], func=AF.Sqrt), reads=[krs], writes=[krs])
                        sc.op(dve, lambda e, RS=RS: e.reciprocal(out=RS[:], in_=RS[:]), reads=[krs], writes=[krs])
                        sc.op(dve, lambda e, OB=OB, RS=RS: e.tensor_tensor(out=sq[:], in0=OB[:], in1=RS[:].unsqueeze(2).broadcast_to([128, 8, 128]), op=ALU.mult),
                              reads=[kob, krs], writes=["sq"])
                        sc.op(dve, lambda e, ON=ON: e.tensor_tensor(out=ON[:], in0=sq[:], in1=subG[:].unsqueeze(1).broadcast_to([128, 8, 128]), op=ALU.mult),
                              reads=["sq", "subG"], writes=[kon])
                        pb = 1 + n2
                        pvw = PS[pb // 2][:].bitcast(BF16)[:, (pb % 2) * 1024:(pb % 2) * 1024 + 1024]

                        def trs(e, ON=ON, pvw=pvw):
                            for h in range(NH):
                                ins = e.transpose(out=pvw[:, h * 128:(h + 1) * 128], in_=ON[:, h, :], identity=ident[:])
                            return ins
                        sc.op(pe, trs, reads=[kon, "ident"], writes=[bkey(pb)])
                        sc.op(act, lambda e, M=M, tb=tb, pvw=pvw: e.activation(out=M[:, 0:8, tb * 128:(tb + 1) * 128],
                                                                              in_=pvw.rearrange("p (h t) -> p h t", h=NH), func=AF.Identity),
                              reads=[bkey(pb)], writes=[("mixA", i)])
                    sc.op(act, lambda e: e.activation(out=ysq[:], in_=yt[:], func=AF.Square), reads=["yt"], writes=["ysq"])

                    def ssm(e):
                        for j in range(8):
                            ins = e.matmul(bank(0), lhsT=ones[:], rhs=ysq[:, j, :], start=(j == 0), stop=(j == 7))
                        return ins
                    sc.op(pe, ssm, reads=["ysq", "ones"], writes=[bkey(0)])
                    sc.op(act, lambda e: e.activation(out=sd[:], in_=bank(0), func=AF.Sqrt, bias=EPS, scale=1.0 / LW),
                          reads=[bkey(0)], writes=["sd"])
                    sc.op(dve, lambda e: e.reciprocal(out=sd[:], in_=sd[:]), reads=["sd"], writes=["sd"])
                    for j in range(8):
                        sc.op(dve, lambda e, j=j, M=M: e.scalar_tensor_tensor(out=M[:, 8 + j, :], in0=yt[:, j, :], scalar=lng[:, j:j + 1], in1=sd[:],
                                                                            op0=ALU.mult, op1=ALU.mult),
                              reads=["yt", "sd", "lru_norm"], writes=[("mixL", i)])

                prep_tile(0)
                xcnt = 0
                for tt in range(NT):
                    i = tt % 2
                    if tt + 1 < NT:
                        prep_tile(tt + 1)
                    M = mixT[i]
                    for tb in range(4):
                        b = tt * 4 + tb
                        xi = xcnt % 2
                        xcnt += 1
                        XB = xb_[xi]
                        sc.dma(sp, XB[:], x_src[b * 128:(b + 1) * 128, :], reads=[(xkey, b)], writes=[("xb", xi)])
                        for dq in range(4):
                            slot, wkey = slots[dq]
                            W = slot[:].rearrange("p (c f) -> p c f", c=DC)
                            pb = 4 + (tb * 4 + dq) % 4

                            def mmo(e, tb=tb, W=W, pb=pb, M=M):
                                for c in range(DC):
                                    ins = e.matmul(bank(pb), lhsT=M[:, c, tb * 128:(tb + 1) * 128], rhs=W[:, c, :],
                                                   start=(c == 0), stop=(c == DC - 1))
                                return ins
                            sc.op(pe, mmo, reads=[wkey, ("mixA", i), ("mixL", i)], writes=[bkey(pb)])
                            sc.op(dve, lambda e, XB=XB, dq=dq, pb=pb: e.tensor_tensor(out=XB[:, dq * 512:(dq + 1) * 512], in0=bank(pb),
                                                                                     in1=XB[:, dq * 512:(dq + 1) * 512], op=ALU.add),
                                  reads=[bkey(pb), ("xb", xi)], writes=[("xb", xi)])
                        sc.dma(sp, xres[b * 128:(b + 1) * 128, :], XB[:], reads=[("xb", xi)], writes=[("xres", b)])
                w_release(4)
                sc.barrier()

        def phase_ffn_up(l):
            TH = THF
            NHALF, TPH, BPH = S // TH, TH // 512, TH // 128
            for half in range(NHALF):
                with ExitStack() as ph:
                    hT = sb(ph, "hT", (128, DC, TH), BF16)
                    norm_transpose(xres, "xres", "mlp_norm", l, hT, half, "m", TH)
                    Ub = [sb(ph, f"Ub{i}", (128, 514)) for i in range(4)]
                    Tb = [sb(ph, f"Tb{i}", (128, 512)) for i in range(4)]
                    ggb = [sb(ph, f"ggb{i}", (128, 512)) for i in range(2)]
                    asb = [sb(ph, f"asb{i}", (128, 512), BF16) for i in range(3)]
                    acnt = 0
                    for g in range(24):
                        (slot, wkey), = w_acquire(1)
                        W = slot[:].rearrange("p (c f) -> p c f", c=DC)
                        for tl in range(TPH):
                            tok = slice(tl * 512, (tl + 1) * 512)
                            gt = slice(half * TH + tl * 512, half * TH + (tl + 1) * 512)
                            first_tile = (half == 0 and tl == 0)
                            pbase = 4 * (tl % 2)
                            for q in range(4):
                                def mmu(e, q=q, tok=tok, pbase=pbase):
                                    for c in range(DC):
                                        ins = e.matmul(bank(pbase + q), lhsT=W[:, c, q * 128:(q + 1) * 128], rhs=hT[:, c, tok],
                                                       start=(c == 0), stop=(c == DC - 1))
                                    return ins
                                sc.op(pe, mmu, reads=[wkey, ("hT", tl)], writes=[bkey(pbase + q)])
                            for q in range(4):
                                ch = (2 * g + q) if q < 2 else (48 + 2 * g + q - 2)
                                Uq, Tq = Ub[q], Tb[q]
                                uk, tk = ("Ub", q), ("Tb", q)
                                hk = ("fh", ch)
                                if first_tile:
                                    sc.op(pool, lambda e, Uq=Uq: e.memset(Uq[:, 0:2], 0.0), writes=[uk])
                                else:
                                    sc.op(pool, lambda e, Uq=Uq, ch=ch: e.tensor_copy(out=Uq[:, 0:2], in_=fhalo[:, ch, :]), reads=[hk], writes=[uk])
                                sc.op(act, lambda e, Uq=Uq, q=q, pbase=pbase: e.activation(out=Uq[:, 2:514], in_=bank(pbase + q), func=AF.Identity),
                                      reads=[bkey(pbase + q)], writes=[uk])
                                sc.op(act, lambda e, Tq=Tq, q=q, ch=ch, pbase=pbase: e.activation(out=Tq[:], in_=bank(pbase + q), func=AF.Identity,
                                                                                               bias=fcb[:, ch:ch + 1], scale=fcw[:, ch, 2:3]),
                                      reads=[bkey(pbase + q), "fcw", "fcb"], writes=[tk])
                                for k in range(2):
                                    sc.op(dve, lambda e, Uq=Uq, Tq=Tq, ch=ch, k=k: e.scalar_tensor_tensor(out=Tq[:], in0=Uq[:, k:k + 512], scalar=fcw[:, ch, k:k + 1], in1=Tq[:],
                                                                                                        op0=ALU.mult, op1=ALU.add), reads=[uk, tk, "fcw"], writes=[tk])
                                sc.op(pool, lambda e, Uq=Uq, ch=ch: e.tensor_copy(out=fhalo[:, ch, :], in_=Uq[:, 512:514]), reads=[uk], writes=[hk])
                            for jj in range(2):
                                sc.op(act, lambda e, jj=jj: e.activation(out=ggb[jj][:], in_=Tb[jj][:], func=AF.Gelu),
                                      reads=[("Tb", jj)], writes=[("ggb", jj)])
                                o = acnt % 3
                                acnt += 1
                                sc.op(pool, lambda e, jj=jj, o=o: e.tensor_tensor(out=asb[o][:], in0=ggb[jj][:], in1=Tb[2 + jj][:], op=ALU.mult),
                                      reads=[("ggb", jj), ("Tb", 2 + jj)], writes=[("asb", o)])
                                sc.dma(sp, act_d[2 * g + jj, :, gt], asb[o][:], reads=[("asb", o)], writes=[("act_d", 2 * g + jj)])
                        w_release(1)
                    sc.barrier(keep_prefix=("w", "ps", "hcar", "halo_d", "fh"))

        def phase_ffn_down(l):
            with ExitStack() as ph:
                aT = sb(ph, "aT", (128, 48, 512), BF16)
                xb_ = [sb(ph, f"xd{i}", (128, D)) for i in range(4)]

                def load_act(tt, piece):
                    tok = slice(tt * 512, (tt + 1) * 512)
                    f0 = piece * 24
                    for f1 in range(f0, f0 + 24, 8):
                        sc.dma(sp, aT[:, f1:f1 + 8, :], act_d[f1:f1 + 8, :, tok].rearrange("f p t -> p f t"),
                               reads=[("act_d", f) for f in range(f1, f1 + 8)], writes=[("aT", piece)])
                load_act(0, 0)
                load_act(0, 1)
                for tt in range(NT):
                    xs = [xb_[tb] for tb in range(4)]
                    xk = [("xd", tb) for tb in range(4)]
                    for tb in range(4):
                        b = tt * 4 + tb
                        sc.dma(sp, xs[tb][:], xres[b * 128:(b + 1) * 128, :], reads=[("xres", b)], writes=[xk[tb]])
                    for dh in range(2):
                        for fg in range(6):
                            (slot, wkey), = w_acquire(1)
                            W = slot[:].rearrange("p (f d) -> p f d", f=8)
                            piece = fg // 3

                            def mmd(e, fg=fg, W=W):
                                for f in range(8):
                                    fa = fg * 8 + f
                                    for tb in range(4):
                                        for dq in range(2):
                                            ins = e.matmul(bank(tb * 2 + dq), lhsT=aT[:, fa, tb * 128:(tb + 1) * 128],
                                                           rhs=W[:, f, dq * 512:(dq + 1) * 512],
                                                           start=(fa == 0), stop=(fa == 47))
                                return ins
                            sc.op(pe, mmd, reads=[wkey, ("aT", piece)], writes=[bkey(i) for i in range(8)])
                            w_release(1)
                            if dh == 1 and fg == 2 and tt + 1 < NT:
                                load_act(tt + 1, 0)
                        for tb in range(4):
                            for dq in range(2):
                                cs = slice(dh * 1024 + dq * 512, dh * 1024 + (dq + 1) * 512)
                                sc.op(dve, lambda e, tb=tb, dq=dq, cs=cs, xs=xs: e.tensor_tensor(out=xs[tb][:, cs], in0=bank(tb * 2 + dq), in1=xs[tb][:, cs], op=ALU.add),
                                      reads=[bkey(tb * 2 + dq), xk[tb]], writes=[xk[tb]])
                    if tt + 1 < NT:
                        load_act(tt + 1, 1)
                    for tb in range(4):
                        b = tt * 4 + tb
                        sc.dma(sp, xres[b * 128:(b + 1) * 128, :], xs[tb][:], reads=[xk[tb]], writes=[("xres", b)])
                sc.barrier(keep_prefix=("w", "ps", "hcar", "halo_d", "fh"))

        def phase_final_norm(x_src):
            with ExitStack() as ph:
                xt = [sb(ph, f"xf{i}", (128, D)) for i in range(2)]
                xo = [sb(ph, f"xo{i}", (128, D)) for i in range(2)]
                st = [sb(ph, f"sf{i}", (128, 4)) for i in range(2)]
                junkf = sb(ph, "junkf", (128, D), BF16)
                sc.dma(sp, gbc[:], prm["final_norm"].rearrange("(o d) -> o d", o=1).partition_broadcast(128), writes=["gbc"])
                for b in range(NB):
                    i = b % 2
                    X, XO, ST = xt[i], xo[i], st[i]
                    sc.dma(sp, X[:], x_src[b * 128:(b + 1) * 128, :], reads=[("xres", b)], writes=[("xf", i)])
                    sc.op(act, lambda e, X=X, ST=ST: e.activation(out=junkf[:], in_=X[:], func=AF.Square, accum_out=ST[:, 0:1]),
                          reads=[("xf", i)], writes=["junkf", ("sf", i)])
                    sc.op(dve, lambda e, ST=ST: e.tensor_scalar(out=ST[:, 1:2], in0=ST[:, 0:1], scalar1=1.0 / D, scalar2=EPS, op0=ALU.mult, op1=ALU.add),
                          reads=[("sf", i)], writes=[("sf", i)])
                    sc.op(act, lambda e, ST=ST: e.activation(out=ST[:, 2:3], in_=ST[:, 1:2], func=AF.Sqrt), reads=[("sf", i)], writes=[("sf", i)])
                    sc.op(dve, lambda e, ST=ST: e.reciprocal(out=ST[:, 3:4], in_=ST[:, 2:3]), reads=[("sf", i)], writes=[("sf", i)])
                    sc.op(dve, lambda e, X=X, XO=XO, ST=ST: e.scalar_tensor_tensor(out=XO[:], in0=X[:], scalar=ST[:, 3:4], in1=gbc[:], op0=ALU.mult, op1=ALU.mult),
                          reads=[("xf", i), ("sf", i), "gbc"], writes=[("xo", i)])
                    sc.dma(sp, out[b * 128:(b + 1) * 128, :], XO[:], reads=[("xo", i)], writes=[("out", b)])

        hcar = sb(es, "hcar", (128, 8))
        fhalo = sb(es, "fhalo", (128, 96, 2))
        halo_d = dscr("halo_d", (8, 128, 3), F32)

        x_src, xkey = (x_in, "xin") if first else (xres, "xres")
        for li, l in enumerate(layers):
            if li > 0:
                sc.rotate()
            if stop == "rope":
                break
            load_layer_params(l)
            if stop == "params":
                break
            phase_mixer_proj(l, x_src, xkey)
            if stop == "mixer":
                break
            phase_attention(l)
            if stop == "attn":
                break
            phase_out_proj(l, x_src, xkey)
            x_src, xkey = xres, "xres"
            if stop == "outproj":
                break
            phase_ffn_up(l)
            if stop == "ffnup":
                break
            phase_ffn_down(l)
        if stop is not None:
            pass
        elif do_final:
            phase_final_norm(x_src)
        else:
            with ExitStack() as ph:
                xt = [sb(ph, f"xc{i}", (128, D)) for i in range(2)]
                for b in range(NB):
                    i = b % 2
                    sc.dma(sp, xt[i][:], x_src[b * 128:(b + 1) * 128, :], reads=[("xres", b)], writes=[("xcp", i)])
                    sc.dma(sp, out[b * 128:(b + 1) * 128, :], xt[i][:], reads=[("xcp", i)], writes=[("out", b)])
        sc.final_wait()
        build_program.stats = (sc.nops, sc.nwait)
    return nc


def host_consts():
    ident = np.eye(128, dtype=np.float32)
    perm = np.zeros((128, 128), np.float32)
    for m in range(128):
        half = (m % 64) // 32
        partner = m + 32 if half == 0 else m - 32
        perm[partner, m] = 1.0
    k = np.arange(128)[:, None]
    q = np.arange(128)[None, :]
    tri = (q >= k).astype(np.float32)
    inv_freq = (1.0 / (np.float32(10000.0) ** (np.arange(0, 64, 2, dtype=np.float32) / np.float32(64)))).astype(np.float32)
    vec = np.zeros((128, 2), np.float32)
    for p in range(128):
        vec[p, 0] = inv_freq[p % 32]
        vec[p, 1] = -1.0 if (p % 64) < 32 else 1.0
    return {"c_ident": ident, "c_perm": perm, "c_tri": tri, "c_vec": vec}


N_CORES = 8
FUSED = True


def kernel(**inputs):
    x = np.ascontiguousarray(inputs["x"], dtype=np.float32)
    positions = np.ascontiguousarray(inputs["positions"], dtype=np.int32)
    B, S, _ = x.shape
    params = {k: np.ascontiguousarray(inputs[k], dtype=np.float32) for k in PARAM_SHAPES}
    consts = host_consts()
    if FUSED:
        plan = [(list(range(DEPTH)), True, True)]
    else:
        plan = [([l], l == 0, l == DEPTH - 1) for l in range(DEPTH)]
    cur = [x[b] for b in range(B)]
    for layers, first, do_final in plan:
        nc = build_program(S, layers, first, do_final)
        in_maps = []
        for b in range(B):
            m = {"x": cur[b], "pos": positions[b:b + 1]}
            m.update(params)
            m.update(consts)
            in_maps.append(m)
        res = run_bass_kernel_spmd(nc, in_maps, core_ids=list(range(N_CORES)))
        cur = [np.asarray(res.results[b]["out"], dtype=np.float32) for b in range(B)]
    return np.stack(cur, axis=0)
```

```python
import math
from contextlib import ExitStack

import numpy as np
import concourse.bass as bass
import concourse.mybir as mybir
from concourse.bass_utils import run_bass_kernel_spmd

F32 = mybir.dt.float32
BF16 = mybir.dt.bfloat16
I32 = mybir.dt.int32
AF = mybir.ActivationFunctionType
ALU = mybir.AluOpType
AX = mybir.AxisListType

D = 2048
DC = 16
DEPTH = 4
NH = 8
LW = 1024
DFF = 6144
INW = 5120
EPS = 1e-6
TWO_PI = 2.0 * math.pi

PARAM_SHAPES = {
    "attn_norm": (DEPTH, D), "w_in": (DEPTH, D, INW),
    "lambda_q1": (DEPTH, 64), "lambda_k1": (DEPTH, 64), "lambda_q2": (DEPTH, 64), "lambda_k2": (DEPTH, 64),
    "subln": (DEPTH, 128), "lru_conv_w": (DEPTH, 4, LW), "lru_conv_b": (DEPTH, LW),
    "gate_a_w": (DEPTH, 8, 128, 128), "gate_a_b": (DEPTH, 8, 128),
    "gate_x_w": (DEPTH, 8, 128, 128), "gate_x_b": (DEPTH, 8, 128),
    "lru_lambda": (DEPTH, LW), "lru_norm": (DEPTH, LW), "w_out": (DEPTH, D, D),
    "mlp_norm": (DEPTH, D), "w_up": (DEPTH, D, 2 * DFF), "ffn_conv_w": (DEPTH, 3, 2 * DFF),
    "ffn_conv_b": (DEPTH, 2 * DFF), "w_down": (DEPTH, DFF, D), "final_norm": (D,),
}


def lambda_init_fn(layer_idx):
    return 0.8 - 0.6 * math.exp(-0.3 * layer_idx)


class Sched:
    ENGS = ("pe", "act", "dve", "pool", "sp")
    NCLK = 96

    def __init__(self, nc, es):
        self.nc = nc
        self.es = es
        self.eng = {"pe": nc.tensor, "act": nc.scalar, "dve": nc.vector, "pool": nc.gpsimd, "sp": nc.sync}
        self.sems = []
        self.cnt = []
        self.cur = {}
        self.known = {e: np.zeros(self.NCLK, np.int64) for e in self.ENGS}
        self.snap = {}
        self.buf = {}
        self.nwait = 0
        self.nops = 0
        self.rotate()
        self.rings = {}
        for q, k in (("sp", 12), ("pool", 6), ("act", 4)):
            clks = [self._new_clock(f"d_{q}{i}") for i in range(k)]
            self.rings[q] = {"clks": clks, "i": 0}

    def _new_clock(self, name):
        sem = self.es.enter_context(self.nc.semaphore(name))
        self.sems.append(sem)
        self.cnt.append(0)
        assert len(self.sems) <= self.NCLK
        return len(self.sems) - 1

    def rotate(self):
        n = len(self.sems)
        for e in self.ENGS:
            self.cur[e] = self._new_clock(f"e_{e}{n}")

    def _deps(self, reads, writes, own=None):
        deps = {}

        def add(cv):
            c, v = cv
            if deps.get(c, 0) < v:
                deps[c] = v
        for k in reads:
            b = self.buf.get(k)
            if b is not None and b[0] is not None:
                add(b[0])
            if b is not None and isinstance(k, tuple) and k[0] == "ps":
                for cv in b[1].items():
                    if cv[0] != own:
                        add(cv)
        for k in writes:
            b = self.buf.get(k)
            if b is not None:
                if b[0] is not None:
                    add(b[0])
                for cv in b[1].items():
                    add(cv)
        return deps

    def _wait(self, E, deps):
        kn = self.known[E]
        for c, v in sorted(deps.items(), key=lambda cv: -cv[1]):
            if kn[c] >= v:
                continue
            if E == "pe" and c == self.cur["pe"]:
                continue
            self.eng[E].wait_ge(self.sems[c], int(v))
            self.nwait += 1
            np.maximum(kn, self.snap[(c, v)], out=kn)

    def _commit(self, c, v, E, reads, writes):
        s = self.known[E].copy()
        s[c] = v
        self.snap[(c, v)] = s
        for k in reads:
            b = self.buf.get(k)
            if b is None:
                b = self.buf[k] = [None, {}]
            b[1][c] = v
        for k in writes:
            self.buf[k] = [(c, v), {}]

    def op(self, E, fn, reads=(), writes=()):
        self._wait(E, self._deps(reads, writes, own=self.cur[E]))
        ins = fn(self.eng[E])
        c = self.cur[E]
        self.cnt[c] += 1
        v = self.cnt[c]
        ins.then_inc(self.sems[c], 1)
        self.nops += 1
        self._commit(c, v, E, reads, writes)

    def dma(self, Q, out, in_, reads=(), writes=()):
        ring = self.rings[Q]
        i = ring["i"]
        ring["i"] += 1
        k = len(ring["clks"])
        c = ring["clks"][i % k]
        rnd = i // k
        deps = self._deps(reads, writes)
        if rnd > 0 and deps.get(c, 0) < 16 * rnd:
            deps[c] = 16 * rnd
        self._wait(Q, deps)
        ins = self.eng[Q].dma_start(out=out, in_=in_)
        ins.then_inc(self.sems[c], 16)
        v = 16 * (rnd + 1)
        self.cnt[c] = v
        self._commit(c, v, Q, reads, writes)

    def barrier(self, keep_prefix=("w",)):
        wclks = set(self.rings["pool"]["clks"])
        deps = {c: v for c, v in enumerate(self.cnt) if v > 0 and c not in wclks}
        for E in self.ENGS:
            self._wait(E, dict(deps))
        nb = {}
        for k, b in self.buf.items():
            w = b[0] if (b[0] is not None and b[0][0] in wclks) else None
            r = {c: v for c, v in b[1].items() if c in wclks}
            if w is not None or r:
                nb[k] = [w, r]
        self.buf = nb

    def final_wait(self):
        deps = {c: v for c, v in enumerate(self.cnt) if v > 0}
        for E in self.ENGS:
            self._wait(E, dict(deps))


def build_program(S, layers, first, do_final, dbg=False, stop=None):
    nc = bass.Bass("TRN2", target_bir_lowering=False)
    NB = S // 128
    NT = S // 512
    THM = min(1024, S)
    THF = min(2048, S)

    def din(name, shape, dt=F32):
        return nc.dram_tensor(name, list(shape), dt, kind="ExternalInput").ap()

    x_in = din("x", (S, D))
    pos_in = din("pos", (1, S), I32)
    prm = {k: din(k, shp) for k, shp in PARAM_SHAPES.items()}
    c_ident = din("c_ident", (128, 128))
    c_perm = din("c_perm", (128, 128))
    c_tri = din("c_tri", (128, 128))
    c_vec = din("c_vec", (128, 2))
    out = nc.dram_tensor("out", [S, D], F32, kind="ExternalOutput").ap()

    kind_scr = "ExternalOutput" if dbg else "Internal"

    def dscr(name, shape, dt):
        return nc.dram_tensor(name, list(shape), dt, kind=kind_scr).ap()

    xres = dscr("xres", (S, D), F32)
    qT_d = dscr("qT_d", (NH, 128, S), BF16)
    kT_d = dscr("kT_d", (NH, 128, S), BF16)
    v_d = dscr("v_d", (NH, 128, NB, 128), BF16)
    y_d = dscr("y_d", (8, 128, S), F32)
    at_d = dscr("at_d", (NH, 128, S), BF16)
    o_d = dscr("o_d", (NB, 128, NH, 128), F32)
    act_d = dscr("act_d", (48, 128, S), BF16)
    cos_d = dscr("cos_d", (128, S), F32)
    sin_d = dscr("sin_d", (128, S), F32)

    es = ExitStack()
    with es:
        sc = Sched(nc, es)
        es.enter_context(nc.allow_non_contiguous_dma(reason="small strided parameter loads"))
        es.enter_context(nc.allow_low_precision(reason="bf16 matmul operands, fp32 accumulation"))

        uniq = [0]

        def sb(stack, name, shape, dt=F32):
            uniq[0] += 1
            return stack.enter_context(nc.sbuf_tensor(f"{name}_{uniq[0]}", list(shape), dt))

        PS = [es.enter_context(nc.psum_tensor(f"ps{i}", [128, 1024], F32)) for i in range(4)]

        def bank(i):
            return PS[i // 2][:, (i % 2) * 512:(i % 2) * 512 + 512]

        def bkey(i):
            return ("ps", i)

        ident = sb(es, "ident", (128, 128), BF16)
        perm = sb(es, "perm", (128, 128), BF16)
        tri = sb(es, "tri", (128, 128), BF16)
        ones = sb(es, "ones", (128, 128), BF16)
        identf = sb(es, "identf", (128, 128))
        pstage = [sb(es, f"pstage{i}", (128, 128)) for i in range(2)]
        cvec = sb(es, "cvec", (128, 2))
        RING = 4
        wring = [sb(es, f"wring{i}", (128, 8192), BF16) for i in range(RING)]
        gbc = sb(es, "gbc", (128, D))
        lamt = sb(es, "lamt", (128, 8))
        lq = sb(es, "lq", (128, 4, 64))
        neglam = sb(es, "neglam", (128, 1))
        subG = sb(es, "subG", (128, 128))
        cw = sb(es, "cw", (128, 8, 4))
        cb = sb(es, "cb", (128, 8))
        gab = sb(es, "gab", (128, 8))
        gxb = sb(es, "gxb", (128, 8))
        lru_l = sb(es, "lru_l", (128, 8))
        hc = sb(es, "hc", (128, 8))
        hc2 = sb(es, "hc2", (128, 8))
        lng = sb(es, "lng", (128, 8))
        gaw = sb(es, "gaw", (128, 8, 128), BF16)
        gxw = sb(es, "gxw", (128, 8, 128), BF16)
        fcw = sb(es, "fcw", (128, 96, 3))
        fcb = sb(es, "fcb", (128, 96))
        tmp8 = [sb(es, f"tmp8_{i}", (128, 8)) for i in range(6)]

        dve, act, pe, pool, sp = "dve", "act", "pe", "pool", "sp"

        WL = []
        for l in layers:
            for h in range(S // THM):
                for g in range(10):
                    WL.append(("in", l, g))
            for dq in range(4):
                WL.append(("out", l, dq))
            for h in range(S // THF):
                for g in range(24):
                    WL.append(("up", l, g))
            for tt in range(NT):
                for dh in range(2):
                    for fg in range(6):
                        WL.append(("down", l, dh, fg))
        wstate = {"loaded": 0, "released": 0, "next": 0}

        def w_emit_load(g):
            slot = wring[g % RING]
            key = ("w", g % RING)
            spec = WL[g]
            if spec[0] == "in":
                _, l, gi = spec
                src = prm["w_in"][l].rearrange("(c p) f -> p c f", p=128)
                dst = slot[:].rearrange("p (c f) -> p c f", c=16)
                if gi < 6:
                    sc.dma(pool, dst, src[:, :, gi * 512:(gi + 1) * 512], writes=[key])
                else:
                    j = (gi - 6) * 2
                    sc.dma(pool, dst[:, :, 0:256], src[:, :, 3072 + j * 128:3072 + j * 128 + 256], writes=[key])
                    sc.dma(pool, dst[:, :, 256:512], src[:, :, 4096 + j * 128:4096 + j * 128 + 256], writes=[key])
            elif spec[0] == "out":
                _, l, dq = spec
                src = prm["w_out"][l].rearrange("(c p) f -> p c f", p=128)
                dst = slot[:].rearrange("p (c f) -> p c f", c=16)
                sc.dma(pool, dst, src[:, :, dq * 512:(dq + 1) * 512], writes=[key])
            elif spec[0] == "up":
                _, l, gi = spec
                src = prm["w_up"][l].rearrange("(c p) f -> p c f", p=128)
                dst = slot[:].rearrange("p (c f) -> p c f", c=16)
                j = gi * 2
                sc.dma(pool, dst[:, :, 0:256], src[:, :, j * 128:j * 128 + 256], writes=[key])
                sc.dma(pool, dst[:, :, 256:512], src[:, :, DFF + j * 128:DFF + j * 128 + 256], writes=[key])
            else:
                _, l, dh, fg = spec
                src = prm["w_down"][l].rearrange("(f p) d -> p f d", p=128)
                dst = slot[:].rearrange("p (f d) -> p f d", f=8)
                sc.dma(pool, dst, src[:, fg * 8:(fg + 1) * 8, dh * 1024:(dh + 1) * 1024], writes=[key])

        def w_prefetch():
            lim = min(len(WL), wstate["released"] + RING)
            while wstate["loaded"] < lim:
                w_emit_load(wstate["loaded"])
                wstate["loaded"] += 1

        def w_acquire(n=1):
            g0 = wstate["next"]
            wstate["next"] += n
            assert wstate["released"] == g0
            w_prefetch()
            assert wstate["loaded"] >= g0 + n
            return [(wring[(g0 + i) % RING], ("w", (g0 + i) % RING)) for i in range(n)]

        def w_release(n=1):
            wstate["released"] += n
            w_prefetch()

        sc.dma(pool, ident[:], c_ident, writes=["ident"])
        sc.dma(pool, perm[:], c_perm, writes=["perm"])
        sc.dma(pool, tri[:], c_tri, writes=["tri"])
        sc.dma(sp, cvec[:], c_vec, writes=["cvec"])
        sc.dma(sp, identf[:], c_ident, writes=["identf"])
        sc.op(dve, lambda e: e.memset(ones[:], 1.0), writes=["ones"])
        for i in range(2):
            sc.op(dve, lambda e, i=i: e.memset(pstage[i][:], 0.0), writes=[("pstage", i)])
        w_prefetch()

        with ExitStack() as ph:
            CH = min(S, 2048)
            posi = sb(ph, "posi", (128, CH), I32)
            ang = sb(ph, "ang", (128, CH))
            kf = sb(ph, "kf", (128, CH))
            ki = sb(ph, "ki", (128, CH), I32)
            rr = sb(ph, "rr", (128, CH))
            mm = sb(ph, "mm", (128, CH))
            r2 = sb(ph, "r2", (128, CH))
            C1 = 6.28125
            C2 = TWO_PI - C1
            PI_LO = 3.1415920

            def fix_range(t, tag):
                sc.op(dve, lambda e: e.tensor_single_scalar(out=mm[:], in_=t[:], scalar=math.pi, op=ALU.is_gt),
                      reads=[tag], writes=["mm"])
                sc.op(dve, lambda e: e.scalar_tensor_tensor(out=t[:], in0=mm[:], scalar=-TWO_PI, in1=t[:],
                                                            op0=ALU.mult, op1=ALU.add), reads=["mm", tag], writes=[tag])
                sc.op(dve, lambda e: e.tensor_single_scalar(out=mm[:], in_=t[:], scalar=-math.pi, op=ALU.is_lt),
                      reads=[tag], writes=["mm"])
                sc.op(dve, lambda e: e.scalar_tensor_tensor(out=t[:], in0=mm[:], scalar=TWO_PI, in1=t[:],
                                                            op0=ALU.mult, op1=ALU.add), reads=["mm", tag], writes=[tag])
                sc.op(dve, lambda e: e.tensor_scalar(out=t[:], in0=t[:], scalar1=PI_LO, scalar2=-PI_LO,
                                                     op0=ALU.min, op1=ALU.max), reads=[tag], writes=[tag])

            for ci in range(S // CH):
                cs = slice(ci * CH, (ci + 1) * CH)
                sc.dma(sp, posi[:], pos_in[:, cs].partition_broadcast(128), writes=["posi"])
                sc.op(dve, lambda e: e.tensor_copy(out=ang[:], in_=posi[:]), reads=["posi"], writes=["ang"])
                sc.op(dve, lambda e: e.tensor_scalar(out=ang[:], in0=ang[:], scalar1=cvec[:, 0:1], scalar2=None,
                                                     op0=ALU.mult), reads=["ang", "cvec"], writes=["ang"])
                sc.op(dve, lambda e: e.tensor_scalar(out=kf[:], in0=ang[:], scalar1=1.0 / TWO_PI, scalar2=None,
                                                     op0=ALU.mult), reads=["ang"], writes=["kf"])
                sc.op(dve, lambda e: e.tensor_copy(out=ki[:], in_=kf[:]), reads=["kf"], writes=["ki"])
                sc.op(dve, lambda e: e.tensor_copy(out=kf[:], in_=ki[:]), reads=["ki"], writes=["kf"])
                sc.op(dve, lambda e: e.scalar_tensor_tensor(out=rr[:], in0=kf[:], scalar=-C1, in1=ang[:],
                                                            op0=ALU.mult, op1=ALU.add), reads=["kf", "ang"], writes=["rr"])
                sc.op(dve, lambda e: e.scalar_tensor_tensor(out=rr[:], in0=kf[:], scalar=-C2, in1=rr[:],
                                                            op0=ALU.mult, op1=ALU.add), reads=["kf", "rr"], writes=["rr"])
                fix_range(rr, "rr")
                sc.op(dve, lambda e: e.tensor_scalar(out=r2[:], in0=rr[:], scalar1=math.pi / 2, scalar2=None,
                                                     op0=ALU.add), reads=["rr"], writes=["r2"])
                fix_range(r2, "r2")
                sc.op(act, lambda e: e.activation(out=ang[:], in_=rr[:], func=AF.Sin), reads=["rr"], writes=["ang"])
                sc.op(act, lambda e: e.activation(out=kf[:], in_=r2[:], func=AF.Sin), reads=["r2"], writes=["kf"])
                sc.op(dve, lambda e: e.tensor_scalar(out=ang[:], in0=ang[:], scalar1=cvec[:, 1:2], scalar2=None,
                                                     op0=ALU.mult), reads=["ang", "cvec"], writes=["ang"])
                sc.dma(sp, sin_d[:, cs], ang[:], reads=["ang"], writes=["sin_d"])
                sc.dma(sp, cos_d[:, cs], kf[:], reads=["kf"], writes=["cos_d"])
            sc.barrier()

        def load_layer_params(l):
            lam_init = lambda_init_fn(l)
            for i, nm in enumerate(("lambda_q1", "lambda_k1", "lambda_q2", "lambda_k2")):
                sc.dma(sp, lq[:, i, :], prm[nm][l:l + 1, :].partition_broadcast(128), writes=["lq"])
            sc.dma(sp, subG[:], prm["subln"][l:l + 1, :].partition_broadcast(128), writes=["subG"])
            pcnt = [0]

            def load_T(dst, src, C, key):
                i = pcnt[0] % 2
                pcnt[0] += 1
                sc.dma(sp, pstage[i][0:C, :], src, writes=[("pstage", i)])
                sc.op(pe, lambda e: e.transpose(out=bank(7)[:, 0:128], in_=pstage[i][:, :], identity=identf[:, :]),
                      reads=[("pstage", i), "identf"], writes=[bkey(7)])
                sc.op(dve, lambda e: e.tensor_copy(out=dst, in_=bank(7)[:, 0:C]), reads=[bkey(7)], writes=[key])

            for k in range(4):
                load_T(cw[:, :, k], prm["lru_conv_w"][l, k].rearrange("(c p) -> c p", p=128), 8, "cw")
            for t, nm in ((cb, "lru_conv_b"), (lru_l, "lru_lambda"), (lng, "lru_norm")):
                load_T(t[:], prm[nm][l].rearrange("(c p) -> c p", p=128), 8, nm)
            for t, nm in ((gab, "gate_a_b"), (gxb, "gate_x_b")):
                load_T(t[:], prm[nm][l], 8, nm)
            sc.dma(pool, gaw[:], prm["gate_a_w"][l].rearrange("h i j -> i h j"), writes=["gaw"])
            sc.dma(pool, gxw[:], prm["gate_x_w"][l].rearrange("h i j -> i h j"), writes=["gxw"])
            for k in range(3):
                load_T(fcw[:, :, k], prm["ffn_conv_w"][l, k].rearrange("(c p) -> c p", p=128), 96, "fcw")
            load_T(fcb[:], prm["ffn_conv_b"][l].rearrange("(c p) -> c p", p=128), 96, "fcb")
            sc.op(dve, lambda e: e.tensor_tensor(out=lq[:, 0, :], in0=lq[:, 0, :], in1=lq[:, 1, :], op=ALU.mult),
                  reads=["lq"], writes=["lq"])
            sc.op(dve, lambda e: e.tensor_tensor(out=lq[:, 2, :], in0=lq[:, 2, :], in1=lq[:, 3, :], op=ALU.mult),
                  reads=["lq"], writes=["lq"])
            sc.op(dve, lambda e: e.reduce_sum(out=lamt[:, 0:1], in_=lq[:, 0, :], axis=AX.X), reads=["lq"], writes=["lamt"])
            sc.op(dve, lambda e: e.reduce_sum(out=lamt[:, 1:2], in_=lq[:, 2, :], axis=AX.X), reads=["lq"], writes=["lamt"])
            sc.op(act, lambda e: e.activation(out=lamt[:, 2:4], in_=lamt[:, 0:2], func=AF.Exp), reads=["lamt"], writes=["lamt"])
            sc.op(dve, lambda e: e.scalar_tensor_tensor(out=neglam[:], in0=lamt[:, 3:4], scalar=-lam_init, in1=lamt[:, 2:3],
                                                        op0=ALU.add, op1=ALU.subtract), reads=["lamt"], writes=["neglam"])
            sc.op(dve, lambda e: e.tensor_scalar(out=subG[:], in0=subG[:], scalar1=1.0 - lam_init, scalar2=None, op0=ALU.mult),
                  reads=["subG"], writes=["subG"])
            sc.op(dve, lambda e: e.tensor_scalar(out=gab[:], in0=gab[:], scalar1=0.5, scalar2=None, op0=ALU.mult),
                  reads=["gate_a_b"], writes=["gate_a_b"])
            sc.op(dve, lambda e: e.tensor_scalar(out=gxb[:], in0=gxb[:], scalar1=0.5, scalar2=None, op0=ALU.mult),
                  reads=["gate_x_b"], writes=["gate_x_b"])
            z, w, w2, acc, t5 = tmp8[0], tmp8[1], tmp8[2], tmp8[3], tmp8[4]
            sc.op(act, lambda e: e.activation(out=z[:], in_=lru_l[:], func=AF.Exp, scale=-1.0), reads=["lru_lambda"], writes=["t0"])
            sc.op(dve, lambda e: e.tensor_scalar(out=w[:], in0=z[:], scalar1=2.0, scalar2=None, op0=ALU.add), reads=["t0"], writes=["t1"])
            sc.op(dve, lambda e: e.reciprocal(out=w[:], in_=w[:]), reads=["t1"], writes=["t1"])
            sc.op(dve, lambda e: e.tensor_tensor(out=w[:], in0=w[:], in1=z[:], op=ALU.mult), reads=["t1", "t0"], writes=["t1"])
            sc.op(dve, lambda e: e.tensor_tensor(out=w2[:], in0=w[:], in1=w[:], op=ALU.mult), reads=["t1"], writes=["t2"])
            sc.op(dve, lambda e: e.tensor_scalar(out=acc[:], in0=w2[:], scalar1=1.0 / 13.0, scalar2=1.0 / 11.0,
                                                 op0=ALU.mult, op1=ALU.add), reads=["t2"], writes=["t3"])
            for cf in (1.0 / 9.0, 1.0 / 7.0, 1.0 / 5.0, 1.0 / 3.0, 1.0):
                sc.op(dve, lambda e: e.tensor_tensor(out=acc[:], in0=acc[:], in1=w2[:], op=ALU.mult), reads=["t3", "t2"], writes=["t3"])
                sc.op(dve, lambda e, cf=cf: e.tensor_scalar(out=acc[:], in0=acc[:], scalar1=cf, scalar2=None, op0=ALU.add),
                      reads=["t3"], writes=["t3"])
            sc.op(dve, lambda e: e.tensor_tensor(out=acc[:], in0=acc[:], in1=w[:], op=ALU.mult), reads=["t3", "t1"], writes=["t3"])
            sc.op(dve, lambda e: e.tensor_scalar(out=hc[:], in0=acc[:], scalar1=-8.0, scalar2=None, op0=ALU.mult), reads=["t3"], writes=["hc"])
            sc.op(dve, lambda e: e.tensor_scalar(out=hc2[:], in0=acc[:], scalar1=-16.0, scalar2=None, op0=ALU.mult), reads=["t3"], writes=["hc2"])

        def norm_transpose(x_src, xkey, gname, l, hT, half, tag, TH):
            with ExitStack() as ph:
                _norm_transpose(ph, x_src, xkey, gname, l, hT, half, tag, TH)
                sc.barrier()

        def _norm_transpose(ph, x_src, xkey, gname, l, hT, half, tag, TH):
            BPH = TH // 128
            xt = [sb(ph, f"xt{tag}{i}", (128, D)) for i in range(2)]
            xn = [sb(ph, f"xn{tag}{i}", (128, D), BF16) for i in range(2)]
            st = [sb(ph, f"st{tag}{i}", (128, 4)) for i in range(2)]
            sc.dma(sp, gbc[:], prm[gname][l:l + 1, :].partition_broadcast(128), writes=["gbc"])
            b0 = half * BPH

            def load(bi):
                b = b0 + bi
                sc.dma(sp, xt[bi % 2][:], x_src[b * 128:(b + 1) * 128, :], reads=[(xkey, b)], writes=[("xt", bi % 2)])
            load(0)
            for bi in range(BPH):
                i2 = bi % 2
                if bi + 1 < BPH:
                    load(bi + 1)
                X, XN, ST = xt[i2], xn[i2], st[i2]
                kx, kn, ks = ("xt", i2), ("xn", i2), ("st", i2)
                sc.op(act, lambda e: e.activation(out=XN[:], in_=X[:], func=AF.Square, accum_out=ST[:, 0:1]),
                      reads=[kx], writes=[kn, ks])
                sc.op(dve, lambda e: e.tensor_scalar(out=ST[:, 1:2], in0=ST[:, 0:1], scalar1=1.0 / D, scalar2=EPS,
                                                     op0=ALU.mult, op1=ALU.add), reads=[ks], writes=[ks])
                sc.op(act, lambda e: e.activation(out=ST[:, 2:3], in_=ST[:, 1:2], func=AF.Sqrt), reads=[ks], writes=[ks])
                sc.op(dve, lambda e: e.reciprocal(out=ST[:, 3:4], in_=ST[:, 2:3]), reads=[ks], writes=[ks])
                sc.op(dve, lambda e: e.scalar_tensor_tensor(out=XN[:], in0=X[:], scalar=ST[:, 3:4], in1=gbc[:],
                                                            op0=ALU.mult, op1=ALU.mult), reads=[kx, ks, "gbc"], writes=[kn])
                pb = 2 * i2
                pv = PS[pb // 2][:].bitcast(BF16)

                def tr(e):
                    for c in range(DC):
                        ins = e.transpose(out=pv[:, c * 128:(c + 1) * 128], in_=XN[:, c * 128:(c + 1) * 128], identity=ident[:])
                    return ins
                sc.op(pe, tr, reads=[kn, "ident"], writes=[bkey(pb), bkey(pb + 1)])
                dst = hT[:, :, bi * 128:(bi + 1) * 128]
                srcv = pv.rearrange("p (c t) -> p c t", c=DC)
                ev = act if bi % 2 == 0 else dve
                if ev == act:
                    sc.op(act, lambda e: e.activation(out=dst, in_=srcv, func=AF.Identity),
                          reads=[bkey(pb), bkey(pb + 1)], writes=[("hT", bi // 4)])
                else:
                    sc.op(dve, lambda e: e.tensor_copy(out=dst, in_=srcv),
                          reads=[bkey(pb), bkey(pb + 1)], writes=[("hT", bi // 4)])

        def phase_mixer_proj(l, x_src, xkey):
            TH = THM
            NHALF, TPH, BPH = S // TH, TH // 512, TH // 128
            for half in range(NHALF):
                with ExitStack() as ph:
                    hT = sb(ph, "hT", (128, DC, TH), BF16)
                    norm_transpose(x_src, xkey, "attn_norm", l, hT, half, "a", TH)
                    cosT = sb(ph, "cosT", (128, TH))
                    sinT = sb(ph, "sinT", (128, TH))
                    hs_ = slice(half * TH, (half + 1) * TH)
                    sc.dma(sp, cosT[:], cos_d[:, hs_], reads=["cos_d"], writes=["cosT"])
                    sc.dma(sp, sinT[:], sin_d[:, hs_], reads=["sin_d"], writes=["sinT"])
                    qb = [sb(ph, f"qb{i}", (128, 512), BF16) for i in range(2)]
                    t1 = [sb(ph, f"t1_{i}", (128, 512)) for i in range(2)]
                    t2 = [sb(ph, f"t2_{i}", (128, 512)) for i in range(2)]
                    ob = [sb(ph, f"ob{i}", (128, 512), BF16) for i in range(3)]
                    U = [sb(ph, f"U{i}", (128, 515)) for i in range(2)]
                    if half == 0:
                        pass
                    xc = [sb(ph, f"xc{i}", (128, 512)) for i in range(2)]
                    xc16 = [sb(ph, f"xc16_{i}", (128, 512), BF16) for i in range(2)]
                    ta = [sb(ph, f"ta{i}", (128, 512)) for i in range(2)]
                    ti = [sb(ph, f"ti{i}", (128, 512)) for i in range(2)]
                    av = [sb(ph, f"av{i}", (128, 512)) for i in range(2)]
                    mv = [sb(ph, f"mv{i}", (128, 512)) for i in range(2)]
                    hsb = [sb(ph, f"hsb{i}", (128, 512)) for i in range(2)]
                    gg = [sb(ph, f"gg{i}", (128, 512)) for i in range(4)]
                    lit = [0]
                    ys = [sb(ph, f"ys{i}", (128, 512)) for i in range(3)]
                    cnt = {"ob": 0, "ys": 0, "rope": 0, "bank": 0}
                    pend = []

                    def rope_rest(r, pb, pb2, pbk, pb2k, tok, gt, hd, dstT, g):
                        sc.op(act, lambda e: e.activation(out=qb[r][:], in_=bank(pb), func=AF.Identity),
                              reads=[pbk], writes=[("qb", r)])
                        sc.op(pe, lambda e: e.matmul(bank(pb2), lhsT=perm[:], rhs=qb[r][:], start=True, stop=True),
                              reads=[("qb", r), "perm"], writes=[pb2k])
                        sc.op(dve, lambda e: e.tensor_tensor(out=t1[r][:], in0=bank(pb), in1=cosT[:, tok], op=ALU.mult),
                              reads=[pbk, "cosT"], writes=[("t1", r)])
                        sc.op(dve, lambda e: e.tensor_tensor(out=t2[r][:], in0=bank(pb2), in1=sinT[:, tok], op=ALU.mult),
                              reads=[pb2k, "sinT"], writes=[("t2", r)])
                        o = cnt["ob"] % 3
                        cnt["ob"] += 1
                        sc.op(pool, lambda e: e.tensor_tensor(out=ob[o][:], in0=t1[r][:], in1=t2[r][:], op=ALU.add),
                              reads=[("t1", r), ("t2", r)], writes=[("ob", o)])
                        sc.dma(sp, dstT[hd, :, gt], ob[o][:], reads=[("ob", o)], writes=[("qk_d", g < 2, hd)])
                    import os as _os
                    _ms = int(_os.environ.get("MIXSTOP", "10"))
                    for g in range(10):
                        if g >= _ms:
                            break
                        (slot, wkey), = w_acquire(1)
                        W = slot[:].rearrange("p (c f) -> p c f", c=DC)
                        if g < 4:
                            dstT = qT_d if g < 2 else kT_d
                            for i in range(4):
                                hd = (g % 2) * 4 + i
                                for tl in range(TPH):
                                    tok = slice(tl * 512, (tl + 1) * 512)
                                    gt = slice(half * TH + tl * 512, half * TH + (tl + 1) * 512)
                                    r = cnt["rope"] % 2
                                    cnt["rope"] += 1
                                    pb, pb2 = r, 2 + r
                                    pbk, pb2k = bkey(pb), bkey(pb2)

                                    def mmg(e, i=i, tok=tok, pb=pb):
                                        for c in range(DC):
                                            ins = e.matmul(bank(pb), lhsT=W[:, c, i * 128:(i + 1) * 128], rhs=hT[:, c, tok],
                                                           start=(c == 0), stop=(c == DC - 1))
                                        return ins
                                    sc.op(pe, mmg, reads=[wkey, ("hT", tl)], writes=[pbk])
                                    if pend:
                                        pend.pop()()

                                    def rest(r=r, pb=pb, pb2=pb2, pbk=pbk, pb2k=pb2k, tok=tok, gt=gt, hd=hd, dstT=dstT, g=g):
                                        rope_rest(r, pb, pb2, pbk, pb2k, tok, gt, hd, dstT, g)
                                    pend.append(rest)
                                    continue
                                    sc.op(act, lambda e, r=r, pb=pb: e.activation(out=qb[r][:], in_=bank(pb), func=AF.Identity),
                                          reads=[pbk], writes=[("qb", r)])
                                    sc.op(pe, lambda e, r=r, pb2=pb2: e.matmul(bank(pb2), lhsT=perm[:], rhs=qb[r][:], start=True, stop=True),
                                          reads=[("qb", r), "perm"], writes=[pb2k])
                                    sc.op(dve, lambda e, r=r, pb=pb, tok=tok: e.tensor_tensor(out=t1[r][:], in0=bank(pb), in1=cosT[:, tok], op=ALU.mult),
                                          reads=[pbk, "cosT"], writes=[("t1", r)])
                                    sc.op(dve, lambda e, r=r, pb2=pb2, tok=tok: e.tensor_tensor(out=t2[r][:], in0=bank(pb2), in1=sinT[:, tok], op=ALU.mult),
                                          reads=[pb2k, "sinT"], writes=[("t2", r)])
                                    o = cnt["ob"] % 3
                                    cnt["ob"] += 1
                                    sc.op(dve if _os.environ.get("NOPOOL") else pool, lambda e, r=r, o=o: e.tensor_tensor(out=ob[o][:], in0=t1[r][:], in1=t2[r][:], op=ALU.add),
                                          reads=[("t1", r), ("t2", r)], writes=[("ob", o)])
                                    sc.dma(sp, dstT[hd, :, gt], ob[o][:], reads=[("ob", o)], writes=[("qk_d", g < 2, hd)])
                            while pend:
                                pend.pop()()
                        elif g < 6:
                            for bi in range(BPH):
                                b = half * BPH + bi
                                pb = 4 + (bi % 2)
                                pbk = bkey(pb)

                                def mmv(e, bi=bi, pb=pb):
                                    for c in range(DC):
                                        ins = e.matmul(bank(pb), lhsT=hT[:, c, bi * 128:(bi + 1) * 128], rhs=W[:, c, :],
                                                       start=(c == 0), stop=(c == DC - 1))
                                    return ins
                                sc.op(pe, mmv, reads=[wkey, ("hT", bi // 4)], writes=[pbk])
                                o = cnt["ob"] % 3
                                cnt["ob"] += 1
                                if bi % 2 == 0:
                                    sc.op(act, lambda e, o=o, pb=pb: e.activation(out=ob[o][:], in_=bank(pb), func=AF.Identity),
                                          reads=[pbk], writes=[("ob", o)])
                                else:
                                    sc.op(dve, lambda e, o=o, pb=pb: e.tensor_copy(out=ob[o][:], in_=bank(pb)),
                                          reads=[pbk], writes=[("ob", o)])
                                h0 = (g - 4) * 4
                                sc.dma(sp, v_d[h0:h0 + 4, :, b, :].rearrange("h p d -> p h d"),
                                       ob[o][:].rearrange("p (h d) -> p h d", h=4), reads=[("ob", o)], writes=[("v_d", g)])
                        else:
                            j0 = (g - 6) * 2
                            for tl in range(TPH):
                                tok = slice(tl * 512, (tl + 1) * 512)
                                gt = slice(half * TH + tl * 512, half * TH + (tl + 1) * 512)
                                first_tile = (half == 0 and tl == 0)
                                for q in range(4):
                                    def mml(e, q=q, tok=tok):
                                        for c in range(DC):
                                            ins = e.matmul(bank(q), lhsT=W[:, c, q * 128:(q + 1) * 128], rhs=hT[:, c, tok],
                                                           start=(c == 0), stop=(c == DC - 1))
                                        return ins
                                    sc.op(pe, mml, reads=[wkey, ("hT", tl)], writes=[bkey(q)])
                                par = (lit[0] % 2) * 2
                                lit[0] += 1
                                for jj in range(2):
                                    sc.op(act, lambda e, jj=jj, par=par: e.activation(out=gg[par + jj][:], in_=bank(2 + jj), func=AF.Gelu),
                                          reads=[bkey(2 + jj)], writes=[("gg", par + jj)])
                                for jj in range(2):
                                    j = j0 + jj
                                    Uj = U[jj]
                                    uk = ("U", jj)
                                    if tl == 0:
                                        if first_tile:
                                            sc.op(pool, lambda e, Uj=Uj: e.memset(Uj[:, 0:3], 0.0), writes=[uk])
                                        else:
                                            sc.dma(sp, Uj[:, 0:3], halo_d[j], reads=[("halo_d", j)], writes=[uk])
                                    sc.op(act, lambda e, Uj=Uj, jj=jj: e.activation(out=Uj[:, 3:515], in_=bank(jj), func=AF.Identity),
                                          reads=[bkey(jj)], writes=[uk])
                                    XC = xc[jj]
                                    xk = ("xc", jj)
                                    sc.op(dve, lambda e, Uj=Uj, XC=XC, j=j: e.tensor_scalar(out=XC[:], in0=Uj[:, 3:515], scalar1=cw[:, j, 3:4], scalar2=cb[:, j:j + 1],
                                                                                            op0=ALU.mult, op1=ALU.add), reads=[uk, "cw", "lru_conv_b"], writes=[xk])
                                    for k in range(3):
                                        sc.op(dve, lambda e, Uj=Uj, XC=XC, j=j, k=k: e.scalar_tensor_tensor(out=XC[:], in0=Uj[:, k:k + 512], scalar=cw[:, j, k:k + 1], in1=XC[:],
                                                                                                          op0=ALU.mult, op1=ALU.add), reads=[uk, xk, "cw"], writes=[xk])
                                    sc.op(pool, lambda e, Uj=Uj: e.tensor_copy(out=Uj[:, 0:3], in_=Uj[:, 512:515]), reads=[uk], writes=[uk])
                                    if tl == TPH - 1 and half + 1 < NHALF:
                                        sc.dma(sp, halo_d[j], Uj[:, 0:3], reads=[uk], writes=[("halo_d", j)])
                                    sc.op(pool, lambda e, XC=XC, jj=jj: e.tensor_copy(out=xc16[jj][:], in_=XC[:]), reads=[xk], writes=[("xc16", jj)])
                                    sc.op(pe, lambda e, jj=jj, j=j: e.matmul(bank(4 + jj), lhsT=gaw[:, j, :], rhs=xc16[jj][:], start=True, stop=True),
                                          reads=[("xc16", jj), "gaw"], writes=[bkey(4 + jj)])
                                    sc.op(pe, lambda e, jj=jj, j=j: e.matmul(bank(6 + jj), lhsT=gxw[:, j, :], rhs=xc16[jj][:], start=True, stop=True),
                                          reads=[("xc16", jj), "gxw"], writes=[bkey(6 + jj)])
                                for jj in range(2):
                                    j = j0 + jj
                                    sc.op(act, lambda e, jj=jj, j=j: e.activation(out=ta[jj][:], in_=bank(4 + jj), func=AF.Tanh, bias=gab[:, j:j + 1], scale=0.5),
                                          reads=[bkey(4 + jj), "gate_a_b"], writes=[("ta", jj)])
                                    sc.op(act, lambda e, jj=jj, j=j: e.activation(out=ti[jj][:], in_=bank(6 + jj), func=AF.Tanh, bias=gxb[:, j:j + 1], scale=0.5),
                                          reads=[bkey(6 + jj), "gate_x_b"], writes=[("ti", jj)])
                                    sc.op(act, lambda e, jj=jj, j=j: e.activation(out=av[jj][:], in_=ta[jj][:], func=AF.Exp, bias=hc[:, j:j + 1], scale=hc[:, j:j + 1]),
                                          reads=[("ta", jj), "hc"], writes=[("av", jj)])
                                    sc.op(act, lambda e, jj=jj, j=j: e.activation(out=mv[jj][:], in_=ta[jj][:], func=AF.Exp, bias=hc2[:, j:j + 1], scale=hc2[:, j:j + 1]),
                                          reads=[("ta", jj), "hc2"], writes=[("mv", jj)])
                                for jj in range(2):
                                    sc.op(act, lambda e, jj=jj: e.activation(out=mv[jj][:], in_=mv[jj][:], func=AF.Sqrt, bias=1.0, scale=-1.0),
                                          reads=[("mv", jj)], writes=[("mv", jj)])
                                for jj in range(2):
                                    j = j0 + jj
                                    XC = xc[jj]
                                    xk = ("xc", jj)
                                    sc.op(dve, lambda e, jj=jj, XC=XC: e.scalar_tensor_tensor(out=ti[jj][:], in0=ti[jj][:], scalar=1.0, in1=XC[:], op0=ALU.add, op1=ALU.mult),
                                          reads=[("ti", jj), xk], writes=[("ti", jj)])
                                    sc.op(dve, lambda e, jj=jj: e.scalar_tensor_tensor(out=ti[jj][:], in0=ti[jj][:], scalar=0.5, in1=mv[jj][:], op0=ALU.mult, op1=ALU.mult),
                                          reads=[("ti", jj), ("mv", jj)], writes=[("ti", jj)])
                                    init = 0.0 if first_tile else hcar[:, j:j + 1]
                                    sc.op(dve, lambda e, jj=jj, init=init: e.tensor_tensor_scan(out=hsb[jj][:], data0=av[jj][:], data1=ti[jj][:], initial=init, op0=ALU.mult, op1=ALU.add),
                                          reads=[("av", jj), ("ti", jj), ("hcar", j)], writes=[("hsb", jj)])
                                    sc.op(dve, lambda e, jj=jj, j=j: e.tensor_copy(out=hcar[:, j:j + 1], in_=hsb[jj][:, 511:512]),
                                          reads=[("hsb", jj)], writes=[("hcar", j)])
                                    o = cnt["ys"] % 3
                                    cnt["ys"] += 1
                                    sc.op(pool, lambda e, jj=jj, o=o, par=par: e.tensor_tensor(out=ys[o][:], in0=hsb[jj][:], in1=gg[par + jj][:], op=ALU.mult),
                                          reads=[("hsb", jj), ("gg", par + jj)], writes=[("ys", o)])
                                    sc.dma(sp, y_d[j, :, gt], ys[o][:], reads=[("ys", o)], writes=[("y_d", j)])
                        w_release(1)
                    sc.barrier(keep_prefix=("w", "ps", "hcar", "halo_d"))

        def phase_attention(l):
            with ExitStack() as ph:
                kT = [sb(ph, f"kT{i}", (128, S), BF16) for i in range(2)]
                qz = [[sb(ph, f"qz{i}_{c}", (128, S), BF16) for c in range(2)] for i in range(2)]
                va = [sb(ph, f"va{i}", (128, NB, 129), BF16) for i in range(2)]
                NPB = 4
                pT = [sb(ph, f"pT{i}", (128, 512), BF16) for i in range(NPB)]
                oall = sb(ph, "oall", (128, NB, 128))
                rec = [sb(ph, f"rec{i}", (128, 4)) for i in range(2)]
                o1 = [sb(ph, f"o1_{i}", (128, 128)) for i in range(2)]
                for i in range(2):
                    sc.op(dve, lambda e, i=i: e.memset(va[i][:, :, 128:129], 1.0), writes=[("va", i)])
                    sc.op(dve, lambda e, i=i: e.memset(qz[i][0][64:128, :], 0.0), writes=[("qT", i)])
                    sc.op(dve, lambda e, i=i: e.memset(qz[i][1][0:64, :], 0.0), writes=[("qT", i)])

                def load_head(h):
                    i = h % 2
                    sc.dma(sp, kT[i][:], kT_d[h], reads=[("qk_d", False, h)], writes=[("kT", i)])
                    sc.dma(sp, qz[i][0][0:64, :], qT_d[h, 0:64, :], reads=[("qk_d", True, h)], writes=[("qT", i)])
                    sc.dma(sp, qz[i][1][64:128, :], qT_d[h, 64:128, :], reads=[("qk_d", True, h)], writes=[("qT", i)])
                    sc.dma(sp, va[i][:, :, 0:128], v_d[h], reads=[("v_d", 4 + h // 4)], writes=[("va", i)])

                units = [(h, qt, kb, c) for h in range(NH) for qt in range(NT) for kb in range(4 * qt + 4) for c in range(2)]
                NU = len(units)
                fin = [0]

                def emit_qk(u):
                    h, qt, kb, c = units[u]
                    hi = h % 2
                    j = kb - 4 * qt
                    qlo = j * 128 if j >= 0 else 0
                    bk = u % NPB
                    sc.op(pe, lambda e: e.matmul(bank(bk)[:, qlo:512],
                                                 lhsT=kT[hi][:, kb * 128:(kb + 1) * 128],
                                                 rhs=qz[hi][c][:, qt * 512 + qlo:(qt + 1) * 512],
                                                 start=True, stop=True),
                          reads=[("kT", hi), ("qT", hi)], writes=[bkey(bk)])

                def emit_rest(u):
                    h, qt, kb, c = units[u]
                    if qt == 0 and kb == 0 and c == 0 and h + 1 < NH:
                        load_head(h + 1)
                    hi = h % 2
                    j = kb - 4 * qt
                    diag = j >= 0
                    qlo = j * 128 if diag else 0
                    bk = u % NPB
                    P = pT[bk]
                    pk = ("pT", bk)
                    sc.op(act, lambda e: e.activation(out=P[:, qlo:512], in_=bank(bk)[:, qlo:512], func=AF.Exp, scale=0.125),
                          reads=[bkey(bk)], writes=[pk])
                    if diag:
                        sc.op(pool, lambda e: e.tensor_tensor(out=P[:, qlo:qlo + 128], in0=P[:, qlo:qlo + 128], in1=tri[:], op=ALU.mult),
                              reads=[pk, "tri"], writes=[pk])
                    qb0 = j if diag else 0
                    V = va[hi]

                    def pv(e):
                        for qb in range(qb0, 4):
                            last = (kb == 4 * qt + qb) and c == 1
                            ins = e.matmul(bank(4 + qb)[:, c * 129:(c + 1) * 129],
                                           lhsT=P[:, qb * 128:(qb + 1) * 128], rhs=V[:, kb, :],
                                           start=(kb == 0 and c == 0), stop=last, skip_group_check=True)
                        return ins
                    sc.op(pe, pv, reads=[pk, ("va", hi)], writes=[bkey(4 + qb) for qb in range(qb0, 4)])
                    if diag and c == 1:
                        qb = j
                        gb_ = qt * 4 + qb
                        f2 = fin[0] % 2
                        fin[0] += 1
                        ob_ = bank(4 + qb)
                        ok = bkey(4 + qb)
                        R = rec[f2]
                        rk = ("rec", f2)
                        sums = ob_[:, 0:258].rearrange("p (c d) -> p c d", c=2)[:, :, 128]
                        sc.op(dve, lambda e: e.reciprocal(out=R[:, 0:2], in_=sums), reads=[ok], writes=[rk])
                        sc.op(dve, lambda e: e.tensor_scalar(out=R[:, 2:3], in0=R[:, 1:2], scalar1=neglam[:, 0:1], scalar2=None, op0=ALU.mult),
                              reads=[rk, "neglam"], writes=[rk])
                        sc.op(dve, lambda e: e.tensor_scalar(out=o1[f2][:], in0=ob_[:, 0:128], scalar1=R[:, 0:1], scalar2=None, op0=ALU.mult),
                              reads=[ok, rk], writes=[("o1", f2)])
                        sc.op(dve, lambda e: e.scalar_tensor_tensor(out=oall[:, gb_, :], in0=ob_[:, 129:257], scalar=R[:, 2:3], in1=o1[f2][:],
                                                                    op0=ALU.mult, op1=ALU.add),
                              reads=[ok, rk, ("o1", f2)], writes=[("oall", qt)])
                        if qb == 3:
                            sc.dma(sp, o_d[qt * 4:(qt + 1) * 4, :, h, :].rearrange("b p d -> p b d"), oall[:, qt * 4:(qt + 1) * 4, :],
                                   reads=[("oall", qt)], writes=[("o_d", qt)])

                DEPTH_PIPE = NPB - 1
                load_head(0)
                for u in range(min(DEPTH_PIPE, NU)):
                    emit_qk(u)
                for u in range(NU):
                    if u + DEPTH_PIPE < NU:
                        emit_qk(u + DEPTH_PIPE)
                    emit_rest(u)
                w_prefetch()
                sc.barrier()

        def phase_out_proj(l, x_src, xkey):
            with ExitStack() as ph:
                mixT = [sb(ph, f"mixT{i}", (128, 16, 512), BF16) for i in range(2)]
                yt = sb(ph, "yt", (128, 8, 512))
                ysq = sb(ph, "ysq", (128, 8, 512), BF16)
                sd = sb(ph, "sd", (128, 512))
                xb_ = [sb(ph, f"xb{i}", (128, D)) for i in range(2)]
                obf = [sb(ph, f"obf{i}", (128, 8, 128)) for i in range(2)]
                sq = sb(ph, "sq", (128, 8, 128))
                onb = [sb(ph, f"onb{i}", (128, 8, 128), BF16) for i in range(2)]
                rs8 = [sb(ph, f"rs8_{i}", (128, 8)) for i in range(2)]
                slots = w_acquire(4)

                def part_a(tt, tb):
                    b = tt * 4 + tb
                    n2 = b % 2
                    OB, ON, RS = obf[n2], onb[n2], rs8[n2]
                    kob, kon, krs = ("obf", n2), ("onb", n2), ("rs8", n2)
                    sc.dma(sp, OB[:], o_d[b], reads=[("o_d", tt)], writes=[kob])
                    sc.op(dve, lambda e: e.tensor_tensor(out=sq[:], in0=OB[:], in1=OB[:], op=ALU.mult), reads=[kob], writes=["sq"])
                    sc.op(dve, lambda e: e.reduce_sum(out=RS[:], in_=sq[:], axis=AX.X), reads=["sq"], writes=[krs])
                    sc.op(dve, lambda e: e.tensor_scalar(out=RS[:], in0=RS[:], scalar1=1.0 / 128.0, scalar2=EPS, op0=ALU.mult, op1=ALU.add),
                          reads=[krs], writes=[krs])
                    sc.op(act, lambda e: e.activation(out=RS[:], in_=RS[:], func=AF.Sqrt), reads=[krs], writes=[krs])
                    sc.op(dve, lambda e: e.reciprocal(out=RS[:], in_=RS[:]), reads=[krs], writes=[krs])
                    sc.op(dve, lambda e: e.tensor_tensor(out=sq[:], in0=OB[:], in1=RS[:].unsqueeze(2).broadcast_to([128, 8, 128]), op=ALU.mult),
                          reads=[kob, krs], writes=["sq"])
                    sc.op(dve, lambda e: e.tensor_tensor(out=ON[:], in0=sq[:], in1=subG[:].unsqueeze(1).broadcast_to([128, 8, 128]), op=ALU.mult),
                          reads=["sq", "subG"], writes=[kon])

                def part_b(tt, tb):
                    b = tt * 4 + tb
                    n2 = b % 2
                    i = tt % 2
                    M, ON, kon = mixT[i], onb[n2], ("onb", n2)
                    pb = 1 + n2
                    pvw = PS[pb // 2][:].bitcast(BF16)[:, (pb % 2) * 1024:(pb % 2) * 1024 + 1024]

                    def trs(e):
                        for h in range(NH):
                            ins = e.transpose(out=pvw[:, h * 128:(h + 1) * 128], in_=ON[:, h, :], identity=ident[:])
                        return ins
                    sc.op(pe, trs, reads=[kon, "ident"], writes=[bkey(pb)])
                    sc.op(act, lambda e: e.activation(out=M[:, 0:8, tb * 128:(tb + 1) * 128],
                                                      in_=pvw.rearrange("p (h t) -> p h t", h=NH), func=AF.Identity),
                          reads=[bkey(pb)], writes=[("mixA", i)])

                def part_y1(tt):
                    tok = slice(tt * 512, (tt + 1) * 512)
                    sc.dma(sp, yt[:], y_d[:, :, tok].rearrange("j p t -> p j t"),
                           reads=[("y_d", j) for j in range(8)], writes=["yt"])
                    sc.op(act, lambda e: e.activation(out=ysq[:], in_=yt[:], func=AF.Square), reads=["yt"], writes=["ysq"])

                def part_y2(tt):
                    i = tt % 2
                    M = mixT[i]

                    def ssm(e):
                        for j in range(8):
                            ins = e.matmul(bank(0), lhsT=ones[:], rhs=ysq[:, j, :], start=(j == 0), stop=(j == 7))
                        return ins
                    sc.op(pe, ssm, reads=["ysq", "ones"], writes=[bkey(0)])
                    sc.op(act, lambda e: e.activation(out=sd[:], in_=bank(0), func=AF.Sqrt, bias=EPS, scale=1.0 / LW),
                          reads=[bkey(0)], writes=["sd"])
                    sc.op(dve, lambda e: e.reciprocal(out=sd[:], in_=sd[:]), reads=["sd"], writes=["sd"])
                    for j in range(8):
                        sc.op(dve, lambda e, j=j: e.scalar_tensor_tensor(out=M[:, 8 + j, :], in0=yt[:, j, :], scalar=lng[:, j:j + 1], in1=sd[:],
                                                                       op0=ALU.mult, op1=ALU.mult),
                              reads=["yt", "sd", "lru_norm"], writes=[("mixL", i)])

                part_y1(0)
                for tb in range(4):
                    part_a(0, tb)
                    part_b(0, tb)
                part_y2(0)
                xcnt = 0
                for tt in range(NT):
                    i = tt % 2
                    nxt = tt + 1 < NT
                    M = mixT[i]
                    for tb in range(4):
                        b = tt * 4 + tb
                        xi = xcnt % 2
                        xcnt += 1
                        XB = xb_[xi]
                        sc.dma(sp, XB[:], x_src[b * 128:(b + 1) * 128, :], reads=[(xkey, b)], writes=[("xb", xi)])
                        for dq in range(4):
                            slot, wkey = slots[dq]
                            W = slot[:].rearrange("p (c f) -> p c f", c=DC)
                            pb = 4 + (tb * 4 + dq) % 4

                            def mmo(e, tb=tb, W=W, pb=pb, M=M):
                                for c in range(DC):
                                    ins = e.matmul(bank(pb), lhsT=M[:, c, tb * 128:(tb + 1) * 128], rhs=W[:, c, :],
                                                   start=(c == 0), stop=(c == DC - 1))
                                return ins
                            sc.op(pe, mmo, reads=[wkey, ("mixA", i), ("mixL", i)], writes=[bkey(pb)])
                            sc.op(dve, lambda e, XB=XB, dq=dq, pb=pb: e.tensor_tensor(out=XB[:, dq * 512:(dq + 1) * 512], in0=bank(pb),
                                                                                     in1=XB[:, dq * 512:(dq + 1) * 512], op=ALU.add),
                                  reads=[bkey(pb), ("xb", xi)], writes=[("xb", xi)])
                        sc.dma(sp, xres[b * 128:(b + 1) * 128, :], XB[:], reads=[("xb", xi)], writes=[("xres", b)])
                        if nxt:
                            if tb == 0:
                                part_y1(tt + 1)
                            part_a(tt + 1, tb)
                            if tb >= 1:
                                part_b(tt + 1, tb - 1)
                            if tb == 2:
                                part_y2(tt + 1)
                    if nxt:
                        part_b(tt + 1, 3)
                w_release(4)
                sc.barrier()

        def phase_ffn_up(l):
            TH = THF
            NHALF, TPH, BPH = S // TH, TH // 512, TH // 128
            for half in range(NHALF):
                with ExitStack() as ph:
                    hT = sb(ph, "hT", (128, DC, TH), BF16)
                    norm_transpose(xres, "xres", "mlp_norm", l, hT, half, "m", TH)
                    Ub = [sb(ph, f"Ub{i}", (128, 514)) for i in range(4)]
                    Tb = [sb(ph, f"Tb{i}", (128, 512)) for i in range(4)]
                    ggb = [sb(ph, f"ggb{i}", (128, 512)) for i in range(2)]
                    asb = [sb(ph, f"asb{i}", (128, 512), BF16) for i in range(3)]
                    acnt = 0
                    for g in range(24):
                        (slot, wkey), = w_acquire(1)
                        W = slot[:].rearrange("p (c f) -> p c f", c=DC)
                        for tl in range(TPH):
                            tok = slice(tl * 512, (tl + 1) * 512)
                            gt = slice(half * TH + tl * 512, half * TH + (tl + 1) * 512)
                            first_tile = (half == 0 and tl == 0)
                            pbase = 4 * (tl % 2)
                            for q in range(4):
                                def mmu(e, q=q, tok=tok, pbase=pbase):
                                    for c in range(DC):
                                        ins = e.matmul(bank(pbase + q), lhsT=W[:, c, q * 128:(q + 1) * 128], rhs=hT[:, c, tok],
                                                       start=(c == 0), stop=(c == DC - 1))
                                    return ins
                                sc.op(pe, mmu, reads=[wkey, ("hT", tl)], writes=[bkey(pbase + q)])
                            for q in range(4):
                                ch = (2 * g + q) if q < 2 else (48 + 2 * g + q - 2)
                                Uq, Tq = Ub[q], Tb[q]
                                uk, tk = ("Ub", q), ("Tb", q)
                                hk = ("fh", ch)
                                if first_tile:
                                    sc.op(pool, lambda e, Uq=Uq: e.memset(Uq[:, 0:2], 0.0), writes=[uk])
                                else:
                                    sc.op(pool, lambda e, Uq=Uq, ch=ch: e.tensor_copy(out=Uq[:, 0:2], in_=fhalo[:, ch, :]), reads=[hk], writes=[uk])
                                sc.op(act, lambda e, Uq=Uq, q=q, pbase=pbase: e.activation(out=Uq[:, 2:514], in_=bank(pbase + q), func=AF.Identity),
                                      reads=[bkey(pbase + q)], writes=[uk])
                                sc.op(act, lambda e, Tq=Tq, q=q, ch=ch, pbase=pbase: e.activation(out=Tq[:], in_=bank(pbase + q), func=AF.Identity,
                                                                                               bias=fcb[:, ch:ch + 1], scale=fcw[:, ch, 2:3]),
                                      reads=[bkey(pbase + q), "fcw", "fcb"], writes=[tk])
                                for k in range(2):
                                    sc.op(dve, lambda e, Uq=Uq, Tq=Tq, ch=ch, k=k: e.scalar_tensor_tensor(out=Tq[:], in0=Uq[:, k:k + 512], scalar=fcw[:, ch, k:k + 1], in1=Tq[:],
                                                                                                        op0=ALU.mult, op1=ALU.add), reads=[uk, tk, "fcw"], writes=[tk])
                                sc.op(pool, lambda e, Uq=Uq, ch=ch: e.tensor_copy(out=fhalo[:, ch, :], in_=Uq[:, 512:514]), reads=[uk], writes=[hk])
                            for jj in range(2):
                                sc.op(act, lambda e, jj=jj: e.activation(out=ggb[jj][:], in_=Tb[jj][:], func=AF.Gelu),
                                      reads=[("Tb", jj)], writes=[("ggb", jj)])
                                o = acnt % 3
                                acnt += 1
                                sc.op(pool, lambda e, jj=jj, o=o: e.tensor_tensor(out=asb[o][:], in0=ggb[jj][:], in1=Tb[2 + jj][:], op=ALU.mult),
                                      reads=[("ggb", jj), ("Tb", 2 + jj)], writes=[("asb", o)])
                                sc.dma(sp, act_d[2 * g + jj, :, gt], asb[o][:], reads=[("asb", o)], writes=[("act_d", 2 * g + jj)])
                        w_release(1)
                    sc.barrier(keep_prefix=("w", "ps", "hcar", "halo_d", "fh"))

        def phase_ffn_down(l):
            with ExitStack() as ph:
                aT = sb(ph, "aT", (128, 48, 512), BF16)
                xb_ = [sb(ph, f"xd{i}", (128, D)) for i in range(4)]

                def load_act(tt, piece):
                    tok = slice(tt * 512, (tt + 1) * 512)
                    f0 = piece * 24
                    for f1 in range(f0, f0 + 24, 8):
                        sc.dma(sp, aT[:, f1:f1 + 8, :], act_d[f1:f1 + 8, :, tok].rearrange("f p t -> p f t"),
                               reads=[("act_d", f) for f in range(f1, f1 + 8)], writes=[("aT", piece)])
                load_act(0, 0)
                load_act(0, 1)
                for tt in range(NT):
                    xs = [xb_[tb] for tb in range(4)]
                    xk = [("xd", tb) for tb in range(4)]
                    for tb in range(4):
                        b = tt * 4 + tb
                        sc.dma(sp, xs[tb][:], xres[b * 128:(b + 1) * 128, :], reads=[("xres", b)], writes=[xk[tb]])
                    for dh in range(2):
                        for fg in range(6):
                            (slot, wkey), = w_acquire(1)
                            W = slot[:].rearrange("p (f d) -> p f d", f=8)
                            piece = fg // 3

                            def mmd(e, fg=fg, W=W):
                                for f in range(8):
                                    fa = fg * 8 + f
                                    for tb in range(4):
                                        for dq in range(2):
                                            ins = e.matmul(bank(tb * 2 + dq), lhsT=aT[:, fa, tb * 128:(tb + 1) * 128],
                                                           rhs=W[:, f, dq * 512:(dq + 1) * 512],
                                                           start=(fa == 0), stop=(fa == 47))
                                return ins
                            sc.op(pe, mmd, reads=[wkey, ("aT", piece)], writes=[bkey(i) for i in range(8)])
                            w_release(1)
                            if dh == 1 and fg == 2 and tt + 1 < NT:
                                load_act(tt + 1, 0)
                        for tb in range(4):
                            for dq in range(2):
                                cs = slice(dh * 1024 + dq * 512, dh * 1024 + (dq + 1) * 512)
                                sc.op(dve, lambda e, tb=tb, dq=dq, cs=cs, xs=xs: e.tensor_tensor(out=xs[tb][:, cs], in0=bank(tb * 2 + dq), in1=xs[tb][:, cs], op=ALU.add),
                                      reads=[bkey(tb * 2 + dq), xk[tb]], writes=[xk[tb]])
                    if tt + 1 < NT:
                        load_act(tt + 1, 1)
                    for tb in range(4):
                        b = tt * 4 + tb
                        sc.dma(sp, xres[b * 128:(b + 1) * 128, :], xs[tb][:], reads=[xk[tb]], writes=[("xres", b)])
                sc.barrier(keep_prefix=("w", "ps", "hcar", "halo_d", "fh"))

        def phase_final_norm(x_src):
            with ExitStack() as ph:
                xt = [sb(ph, f"xf{i}", (128, D)) for i in range(2)]
                xo = [sb(ph, f"xo{i}", (128, D)) for i in range(2)]
                st = [sb(ph, f"sf{i}", (128, 4)) for i in range(2)]
                junkf = sb(ph, "junkf", (128, D), BF16)
                sc.dma(sp, gbc[:], prm["final_norm"].rearrange("(o d) -> o d", o=1).partition_broadcast(128), writes=["gbc"])
                for b in range(NB):
                    i = b % 2
                    X, XO, ST = xt[i], xo[i], st[i]
                    sc.dma(sp, X[:], x_src[b * 128:(b + 1) * 128, :], reads=[("xres", b)], writes=[("xf", i)])
                    sc.op(act, lambda e, X=X, ST=ST: e.activation(out=junkf[:], in_=X[:], func=AF.Square, accum_out=ST[:, 0:1]),
                          reads=[("xf", i)], writes=["junkf", ("sf", i)])
                    sc.op(dve, lambda e, ST=ST: e.tensor_scalar(out=ST[:, 1:2], in0=ST[:, 0:1], scalar1=1.0 / D, scalar2=EPS, op0=ALU.mult, op1=ALU.add),
                          reads=[("sf", i)], writes=[("sf", i)])
                    sc.op(act, lambda e, ST=ST: e.activation(out=ST[:, 2:3], in_=ST[:, 1:2], func=AF.Sqrt), reads=[("sf", i)], writes=[("sf", i)])
                    sc.op(dve, lambda e, ST=ST: e.reciprocal(out=ST[:, 3:4], in_=ST[:, 2:3]), reads=[("sf", i)], writes=[("sf", i)])
                    sc.op(dve, lambda e, X=X, XO=XO, ST=ST: e.scalar_tensor_tensor(out=XO[:], in0=X[:], scalar=ST[:, 3:4], in1=gbc[:], op0=ALU.mult, op1=ALU.mult),
                          reads=[("xf", i), ("sf", i), "gbc"], writes=[("xo", i)])
                    sc.dma(sp, out[b * 128:(b + 1) * 128, :], XO[:], reads=[("xo", i)], writes=[("out", b)])

        hcar = sb(es, "hcar", (128, 8))
        fhalo = sb(es, "fhalo", (128, 96, 2))
        halo_d = dscr("halo_d", (8, 128, 3), F32)

        x_src, xkey = (x_in, "xin") if first else (xres, "xres")
        for li, l in enumerate(layers):
            if li > 0:
                sc.rotate()
            if stop == "rope":
                break
            load_layer_params(l)
            if stop == "params":
                break
            phase_mixer_proj(l, x_src, xkey)
            if stop == "mixer":
                break
            phase_attention(l)
            if stop == "attn":
                break
            phase_out_proj(l, x_src, xkey)
            x_src, xkey = xres, "xres"
            if stop == "outproj":
                break
            phase_ffn_up(l)
            if stop == "ffnup":
                break
            phase_ffn_down(l)
        if stop is not None:
            pass
        elif do_final:
            phase_final_norm(x_src)
        else:
            with ExitStack() as ph:
                xt = [sb(ph, f"xc{i}", (128, D)) for i in range(2)]
                for b in range(NB):
                    i = b % 2
                    sc.dma(sp, xt[i][:], x_src[b * 128:(b + 1) * 128, :], reads=[("xres", b)], writes=[("xcp", i)])
                    sc.dma(sp, out[b * 128:(b + 1) * 128, :], xt[i][:], reads=[("xcp", i)], writes=[("out", b)])
        sc.final_wait()
        build_program.stats = (sc.nops, sc.nwait)
    return nc


def host_consts():
    ident = np.eye(128, dtype=np.float32)
    perm = np.zeros((128, 128), np.float32)
    for m in range(128):
        half = (m % 64) // 32
        partner = m + 32 if half == 0 else m - 32
        perm[partner, m] = 1.0
    k = np.arange(128)[:, None]
    q = np.arange(128)[None, :]
    tri = (q >= k).astype(np.float32)
    inv_freq = (1.0 / (np.float32(10000.0) ** (np.arange(0, 64, 2, dtype=np.float32) / np.float32(64)))).astype(np.float32)
    vec = np.zeros((128, 2), np.float32)
    for p in range(128):
        vec[p, 0] = inv_freq[p % 32]
        vec[p, 1] = -1.0 if (p % 64) < 32 else 1.0
    return {"c_ident": ident, "c_perm": perm, "c_tri": tri, "c_vec": vec}


N_CORES = 8
FUSED = True


def kernel(**inputs):
    x = np.ascontiguousarray(inputs["x"], dtype=np.float32)
    positions = np.ascontiguousarray(inputs["positions"], dtype=np.int32)
    B, S, _ = x.shape
    params = {k: np.ascontiguousarray(inputs[k], dtype=np.float32) for k in PARAM_SHAPES}
    consts = host_consts()
    if FUSED:
        plan = [(list(range(DEPTH)), True, True)]
    else:
        plan = [([l], l == 0, l == DEPTH - 1) for l in range(DEPTH)]
    cur = [x[b] for b in range(B)]
    for layers, first, do_final in plan:
        nc = build_program(S, layers, first, do_final)
        in_maps = []
        for b in range(B):
            m = {"x": cur[b], "pos": positions[b:b + 1]}
            m.update(params)
            m.update(consts)
            in_maps.append(m)
        res = run_bass_kernel_spmd(nc, in_maps, core_ids=list(range(N_CORES)))
        cur = [np.asarray(res.results[b]["out"], dtype=np.float32) for b in range(B)]
    return np.stack(cur, axis=0)
```
